# Optimizing a Trainium2 kernel written in Bass

```python
import jax, jax.numpy as jnp
from jax import lax
import numpy as np

D_MODEL = 2048
BATCH = 1
SEQ = 8192
DEPTH = 1

HEAD_DIM = 128
ATTN_WIDTH = D_MODEL // 2
ATTN_HEADS = ATTN_WIDTH // HEAD_DIM
POOL_WIDTH = D_MODEL // 2
POOL_WINDOWS = (2, 4, 8, 16)
POOL_GROUPS = len(POOL_WINDOWS)
POOL_GROUP_WIDTH = POOL_WIDTH // POOL_GROUPS
DILATED_PATTERNS = ((128, 1), (512, 4), (2048, 16))
SUB_BLOCK = 128
D_FF = 4 * D_MODEL
ROPE_THETA = 10000.0
LN_EPS = 1e-5
DEEPNORM_ALPHA = (2.0 * DEPTH) ** 0.25
DEEPNORM_BETA = (8.0 * DEPTH) ** -0.25
IN_SPLITS = (ATTN_WIDTH, 2 * ATTN_WIDTH, 3 * ATTN_WIDTH, 3 * ATTN_WIDTH + POOL_WIDTH,
             3 * ATTN_WIDTH + POOL_WIDTH + D_MODEL)
IN_WIDTH = 3 * ATTN_WIDTH + POOL_WIDTH + 2 * D_MODEL

kernel_name = "dilated_attn_pool_gated_hybrid_deepnorm"


def layer_norm(x, g, b):
    xf = x.astype(jnp.float32)
    mu = jnp.mean(xf, axis=-1, keepdims=True)
    var = jnp.mean(jnp.square(xf - mu), axis=-1, keepdims=True)
    return ((xf - mu) * lax.rsqrt(var + LN_EPS) * g.astype(jnp.float32) + b.astype(jnp.float32)).astype(x.dtype)


def rope(t, positions):
    half = HEAD_DIM // 2
    inv_freq = ROPE_THETA ** (-jnp.arange(half, dtype=jnp.float32) / half)
    ang = positions.astype(jnp.float32)[..., None] * inv_freq
    cos = jnp.cos(ang)[:, :, None, :]
    sin = jnp.sin(ang)[:, :, None, :]
    t1 = t[..., :half].astype(jnp.float32)
    t2 = t[..., half:].astype(jnp.float32)
    return jnp.concatenate([t1 * cos - t2 * sin, t2 * cos + t1 * sin], axis=-1).astype(t.dtype)


def dilated_window_attention(q, k, v, window, dilation):
    B, S, H, Dh = q.shape
    span = window // dilation
    blk = SUB_BLOCK
    assert span <= blk
    unit = dilation * blk
    s_pad = -(-S // unit) * unit
    m_len = s_pad // dilation
    nb = m_len // blk

    def to_blocks(t):
        t = jnp.pad(t, ((0, 0), (0, s_pad - S), (0, 0), (0, 0)))
        t = t.reshape(B, m_len, dilation, H, Dh).transpose(0, 2, 1, 3, 4)
        return t.reshape(B, dilation, nb, blk, H, Dh)

    def with_prev(t):
        prev = jnp.pad(t, ((0, 0), (0, 0), (1, 0), (0, 0), (0, 0), (0, 0)))[:, :, :-1]
        return jnp.concatenate([prev, t], axis=3)

    qb = to_blocks(q)
    kw = with_prev(to_blocks(k))
    vw = with_prev(to_blocks(v))
    s = jnp.einsum('brnqhd,brnkhd->brnhqk', qb, kw,
                   preferred_element_type=jnp.float32) * (HEAD_DIM ** -0.5)
    qi = jnp.arange(blk)[:, None]
    kj = jnp.arange(2 * blk)[None, :]
    dist = qi + blk - kj
    band = (dist >= 0) & (dist <= span)
    valid = band[None] & ((jnp.arange(nb)[:, None, None] > 0) | (kj >= blk)[None])
    s = jnp.where(valid[None, None, :, None], s, -jnp.inf)
    mx = jnp.max(s, axis=-1, keepdims=True)
    p = jnp.exp(s - mx)
    l = jnp.sum(p, axis=-1)
    o = jnp.einsum('brnhqk,brnkhd->brnqhd', p, vw.astype(jnp.float32))
    o = o / jnp.swapaxes(l, 3, 4)[..., None]
    lse = jnp.swapaxes(mx[..., 0] + jnp.log(l), 3, 4)
    o = o.reshape(B, dilation, m_len, H, Dh).transpose(0, 2, 1, 3, 4).reshape(B, s_pad, H, Dh)[:, :S]
    lse = lse.reshape(B, dilation, m_len, H).transpose(0, 2, 1, 3).reshape(B, s_pad, H)[:, :S]
    return o, lse


def pool_mixer(u, w_pool, pool_scale):
    B, S, _ = u.shape
    ug = u.reshape(B, S, POOL_GROUPS, POOL_GROUP_WIDTH)
    pooled = []
    for g, w in enumerate(POOL_WINDOWS):
        xg = ug[:, :, g].astype(jnp.float32)
        c = jnp.cumsum(xg, axis=1)
        c_lag = jnp.pad(c, ((0, 0), (w, 0), (0, 0)))[:, :S]
        count = jnp.minimum(jnp.arange(1, S + 1), w).astype(jnp.float32)[None, :, None]
        pooled.append((c - c_lag) / count - xg)
    p = jnp.stack(pooled, axis=2).astype(u.dtype)
    y = jnp.einsum('bsgc,gcd->bsgd', p, w_pool).reshape(B, S, POOL_WIDTH)
    return y * pool_scale


def setup_inputs(seed: int = 0) -> dict:
    key = jax.random.key(seed)
    ks = jax.random.split(key, 20)
    f32 = jnp.float32

    def nrm(k, shape, fan_in, gain=1.0):
        return jax.random.normal(k, shape, f32) * (gain * fan_in ** -0.5)

    x = jax.random.normal(ks[0], (BATCH, SEQ, D_MODEL), f32)
    positions = (jnp.arange(SEQ, dtype=jnp.int32)[None, :]
                 + jax.random.randint(ks[1], (BATCH, 1), 0, 1024, dtype=jnp.int32))
    w_in = jnp.concatenate([
        nrm(ks[2], (DEPTH, D_MODEL, 2 * ATTN_WIDTH), D_MODEL),
        nrm(ks[3], (DEPTH, D_MODEL, ATTN_WIDTH), D_MODEL, DEEPNORM_BETA),
        nrm(ks[4], (DEPTH, D_MODEL, POOL_WIDTH), D_MODEL),
        nrm(ks[5], (DEPTH, D_MODEL, 2 * D_MODEL), D_MODEL),
    ], axis=-1)
    w_pool = nrm(ks[6], (DEPTH, POOL_GROUPS, POOL_GROUP_WIDTH, POOL_GROUP_WIDTH), POOL_GROUP_WIDTH)
    pool_scale = 1.0 + 0.1 * jax.random.normal(ks[7], (DEPTH, POOL_WIDTH), f32)
    w_branch_attn = nrm(ks[8], (DEPTH, ATTN_WIDTH, D_MODEL), ATTN_WIDTH, DEEPNORM_BETA)
    w_branch_pool = nrm(ks[9], (DEPTH, POOL_WIDTH, D_MODEL), POOL_WIDTH, DEEPNORM_BETA)
    w_out = nrm(ks[10], (DEPTH, D_MODEL, D_MODEL), D_MODEL, DEEPNORM_BETA)
    ln_mix_g = 1.0 + 0.05 * jax.random.normal(ks[11], (DEPTH, D_MODEL), f32)
    ln_mix_b = 0.02 * jax.random.normal(ks[12], (DEPTH, D_MODEL), f32)
    w_ff1 = nrm(ks[13], (DEPTH, D_MODEL, D_FF), D_MODEL, DEEPNORM_BETA)
    w_ff2 = nrm(ks[14], (DEPTH, D_FF, D_MODEL), D_FF, DEEPNORM_BETA)
    ln_ff_g = 1.0 + 0.05 * jax.random.normal(ks[15], (DEPTH, D_MODEL), f32)
    ln_ff_b = 0.02 * jax.random.normal(ks[16], (DEPTH, D_MODEL), f32)
    return {"x": x, "positions": positions, "w_in": w_in, "w_pool": w_pool,
            "pool_scale": pool_scale, "w_branch_attn": w_branch_attn,
            "w_branch_pool": w_branch_pool, "w_out": w_out, "ln_mix_g": ln_mix_g,
            "ln_mix_b": ln_mix_b, "w_ff1": w_ff1, "w_ff2": w_ff2,
            "ln_ff_g": ln_ff_g, "ln_ff_b": ln_ff_b}


def reference(x, positions, w_in, w_pool, pool_scale, w_branch_attn, w_branch_pool, w_out,
              ln_mix_g, ln_mix_b, w_ff1, w_ff2, ln_ff_g, ln_ff_b):
    B, S, _ = x.shape
    for layer in range(DEPTH):
        h = x @ w_in[layer]
        q, k, v, u, gate_attn, gate_pool = jnp.split(h, IN_SPLITS, axis=-1)
        q = rope(q.reshape(B, S, ATTN_HEADS, HEAD_DIM), positions)
        k = rope(k.reshape(B, S, ATTN_HEADS, HEAD_DIM), positions)
        v = v.reshape(B, S, ATTN_HEADS, HEAD_DIM)
        outs, lses = [], []
        for window, dilation in DILATED_PATTERNS:
            o_g, lse_g = dilated_window_attention(q, k, v, window, dilation)
            outs.append(o_g)
            lses.append(lse_g)
        mix_w = jax.nn.softmax(jnp.stack(lses, axis=0), axis=0)
        o_attn = jnp.einsum('pbsh,pbshd->bshd', mix_w, jnp.stack(outs, axis=0))
        y_attn = o_attn.reshape(B, S, ATTN_WIDTH).astype(x.dtype) @ w_branch_attn[layer]
        y_pool = pool_mixer(u, w_pool[layer], pool_scale[layer]) @ w_branch_pool[layer]
        merged = jax.nn.sigmoid(gate_attn) * y_attn + jax.nn.sigmoid(gate_pool) * y_pool
        mix = merged @ w_out[layer]
        x = layer_norm(DEEPNORM_ALPHA * x + mix, ln_mix_g[layer], ln_mix_b[layer])
        f = jnp.square(jax.nn.relu(x @ w_ff1[layer])) @ w_ff2[layer]
        x = layer_norm(DEEPNORM_ALPHA * x + f, ln_ff_g[layer], ln_ff_b[layer])
    return x
```

```python
import contextlib
import numpy as np
import concourse.bass as bass
import concourse.mybir as mybir
from concourse.bass_utils import run_bass_kernel_spmd

F32 = mybir.dt.float32
BF16 = mybir.dt.bfloat16
I32 = mybir.dt.int32
AF = mybir.ActivationFunctionType
ALU = mybir.AluOpType

NCORES = 8
D = 2048
S_TOT = 8192
TOK = 1024
HALO = 2048
WIN = TOK + HALO
NKT = WIN // 128
DFF = 8192
ALPHA = float(2.0 ** 0.25)
LN_EPS = 1e-5
NDELTA = 17
TWO_PI_HI = 6.28125
TWO_PI_LO = 0.0019353071795864769
ARENA_ELEMS = 105472

CV_INVF = 0
CV_VALID = 1
CV_INVC = CV_VALID + NKT
CV_PSC = CV_INVC + 64
CV_LN1G = CV_PSC + 8
CV_LN1B = CV_LN1G + 16
CV_LN2G = CV_LN1B + 16
CV_LN2B = CV_LN2G + 16
CV_N = CV_LN2B + 16


class Sched:
    def __init__(self, nc, sems):
        self.nc = nc
        self.eng = {}
        for name in ("pe", "act", "dve", "pool", "sp"):
            self.eng[name] = dict(cnt=0, ops=[], known={}, log=[])
        self.lastw = {}
        self.reads = {}
        self.semobj = dict(sems)
        self.dmacnt = {}

    def _waits(self, e, reads, writes):
        evs = []
        for t in reads:
            if t in self.lastw:
                evs.append(self.lastw[t])
        for t in writes:
            if t in self.lastw:
                evs.append(self.lastw[t])
            evs.extend(self.reads.get(t, []))
        E = self.eng[e]
        need = {}
        for (sk, v) in evs:
            if E["known"].get(sk, 0) >= v:
                continue
            need[sk] = max(need.get(sk, 0), v)
        for sk, v in need.items():
            E["known"][sk] = v
        return list(need.items())

    def _record(self, ev, reads, writes):
        for t in reads:
            self.reads.setdefault(t, []).append(ev)
        for t in writes:
            self.lastw[t] = ev
            self.reads[t] = []

    def op(self, e, fn, reads=(), writes=()):
        E = self.eng[e]
        waits = self._waits(e, reads, writes)
        if e == "pe":
            waits = [(sk, v) for (sk, v) in waits if sk != "pe"]
        E["cnt"] += 1
        ev = (e, E["cnt"])
        semobj = self.semobj

        def run(h, waits=waits, fn=fn, e=e):
            for sk, v in waits:
                h.wait_ge(semobj[sk], v)
            fn(h).then_inc(semobj[e], 1)
        E["ops"].append(run)
        E["log"].append((list(waits), (e, 1)))
        self._record(ev, reads, writes)
        return ev

    def dma(self, e, semkey, fn, reads=(), writes=(), n=1):
        E = self.eng[e]
        waits = self._waits(e, reads, writes)
        self.dmacnt[semkey] = self.dmacnt.get(semkey, 0) + 16 * n
        ev = (semkey, self.dmacnt[semkey])
        semobj = self.semobj

        def run(h, waits=waits, fn=fn, semkey=semkey):
            for sk, v in waits:
                h.wait_ge(semobj[sk], v)
            for ins in fn(h):
                ins.then_inc(semobj[semkey], 16)
        E["ops"].append(run)
        E["log"].append((list(waits), (semkey, 16 * n)))
        self._record(ev, reads, writes)
        return ev

    def barrier(self):
        evs = [(e, E["cnt"]) for e, E in self.eng.items() if E["cnt"] > 0]
        evs += list(self.dmacnt.items())
        semobj = self.semobj
        for e, E in self.eng.items():
            waits = [(sk, v) for (sk, v) in evs if E["known"].get(sk, 0) < v]
            for sk, v in waits:
                E["known"][sk] = v

            def run(h, waits=waits):
                for sk, v in waits:
                    h.wait_ge(semobj[sk], v)
            E["ops"].append(run)
            E["log"].append((list(waits), None))
        self.lastw.clear()
        self.reads.clear()

    def emit(self, block):
        def mk(name):
            def body(h):
                for r in self.eng[name]["ops"]:
                    r(h)
            return body
        block.tensor(mk("pe"))
        block.scalar(mk("act"))
        block.vector(mk("dve"))
        block.gpsimd(mk("pool"))
        block.sync(mk("sp"))


def build_nc(debug=False):
    nc = bass.Bass("TRN2", target_bir_lowering=False)

    def din(name, shape, dt=F32):
        return nc.dram_tensor(name, list(shape), dt, kind="ExternalInput").ap()

    xT = din("xT", [D, WIN])
    pos = din("pos", [1, WIN], I32)
    w_att = din("w_att", [D, 3072])
    w_u = din("w_u", [D, 1024])
    w_g = din("w_g", [D, 4096])
    w_b = din("w_b", [D, 2048])
    w_out = din("w_out", [D, D])
    w1 = din("w1", [D, DFF])
    w2 = din("w2", [DFF, D])
    w_pool = din("w_pool", [4, 256, 256])
    masks_d = din("masks", [128, NDELTA * 128])
    ident_d = din("ident", [128, 128])
    cvec_d = din("cvec", [128, CV_N])
    outT = nc.dram_tensor("outT", [D, TOK], F32, kind="ExternalOutput").ap()
    dbg = {}
    if debug:
        dbg["OT"] = nc.dram_tensor("dbg_OT", [128, 8 * TOK], BF16, kind="ExternalOutput").ap()
        dbg["yp"] = nc.dram_tensor("dbg_yp", [128, 8 * TOK], BF16, kind="ExternalOutput").ap()
        dbg["mg"] = nc.dram_tensor("dbg_mg", [128, 16 * TOK], BF16, kind="ExternalOutput").ap()
        dbg["x1"] = nc.dram_tensor("dbg_x1", [128, 16 * TOK], F32, kind="ExternalOutput").ap()

    def kview(w):
        return w.rearrange("(k p) c -> p k c", p=128)

    xT_v = kview(xT)
    watt_v, wu_v, wg_v, wb_v, wout_v, w1_v, w2_v = (kview(w) for w in (w_att, w_u, w_g, w_b, w_out, w1, w2))
    outT_v = kview(outT)

    with contextlib.ExitStack() as st:
        arena = st.enter_context(nc.sbuf_tensor("arena", [128, ARENA_ELEMS], BF16))
        psb = [st.enter_context(nc.psum_tensor(f"ps{i}", [128, 512], F32)) for i in range(8)]
        semnames = ["pe", "act", "dve", "pool", "sp", "w0", "w1", "w2", "w3", "xh0", "xh1", "xo0", "xo1", "cst0", "cst1", "cst2", "cvs", "xh16", "misc",
                    "xf0", "xf1", "xf2", "xf3", "out", "dbg"]
        sems = {k: st.enter_context(nc.semaphore(k)) for k in semnames}
        block = st.enter_context(nc.Block())
        S = Sched(nc, sems)

        def carve(off, dims, dt):
            n = int(np.prod(dims))
            esz = 2 if dt == BF16 else 4
            assert off % 4 == 0 and off + n * esz <= ARENA_ELEMS * 2, (off, dims)
            ap = arena[:, off // 2: off // 2 + n * esz // 2]
            if dt != BF16:
                ap = ap.bitcast(dt)
            if len(dims) == 2:
                ap = ap.rearrange("p (a b) -> p a b", b=dims[1])
            elif len(dims) == 3:
                ap = ap.rearrange("p (a b c) -> p a b c", b=dims[1], c=dims[2])
            return ap

        class Bump:
            def __init__(self, base):
                self.o = base

            def take(self, dims, dt):
                n = int(np.prod(dims)) * (2 if dt == BF16 else 4)
                n4 = (n + 31) // 32 * 32
                ap = carve(self.o, dims, dt)
                self.o += n4
                return ap

        P = Bump(0)
        ring = [P.take((4096,), BF16) for _ in range(4)]
        masks = P.take((NDELTA * 128,), BF16)
        ident = P.take((128,), BF16)
        ones_bf = P.take((128,), BF16)
        ones_f = P.take((128,), F32)
        cvec = P.take((CV_N,), F32)
        wpool_sb = P.take((4, 2, 256), BF16)
        assert P.o <= 43520, P.o
        L1 = Bump(43520)
        xTo = L1.take((16, TOK), BF16)
        OT = L1.take((8, TOK), BF16)
        assert L1.o == 92672

        loads = []
        for g in range(4):
            for i in (1, 2, 0):
                loads.append(((16, 256), watt_v[:, :, g * 768 + i * 256: g * 768 + (i + 1) * 256]))
        for i in range(4):
            loads.append(((16, 256), wu_v[:, :, i * 256:(i + 1) * 256]))
        for j in range(16):
            loads.append(((16, 256), wg_v[:, :, j * 256:(j + 1) * 256]))
            loads.append(((16, 128), wb_v[:, :, j * 128:(j + 1) * 128]))
        for hf_ in range(2):
            for jp in range(8):
                loads.append(((16, 256), wout_v[:, :, jp * 256:(jp + 1) * 256]))
        def _w1_loads(hg):
            for i in range(4):
                c0 = hg * 1024 + i * 256
                loads.append(((16, 256), w1_v[:, :, c0:c0 + 256]))

        def _w2_loads(hg):
            for jp in range(8):
                loads.append(((8, 256), w2_v[:, hg * 8:(hg + 1) * 8, jp * 256:(jp + 1) * 256]))
        _w1_loads(0)
        _w1_loads(1)
        _w1_loads(0)
        _w1_loads(1)
        _w2_loads(0)
        for hg in range(2, 8):
            _w1_loads(hg)
            _w2_loads(hg - 1)
        _w2_loads(7)
        _w2_loads(7)

        class WStream:
            def __init__(self):
                self.emitted = 0
                self.released = 0
                self.next_use = 0

            def _emit_load(self, i, after=()):
                dims, src = loads[i]
                s = i % 4
                dst = ring[s][:, 0:dims[0] * dims[1]].rearrange("p (a b) -> p a b", b=dims[1])
                S.dma("pool", f"w{s}", lambda h, dst=dst, src=src: [h.dma_start(out=dst, in_=src)],
                      reads=list(after), writes=[f"w{s}"])

            def topup(self, cap=4):
                while self.emitted < min(self.released + cap, len(loads)):
                    self._emit_load(self.emitted)
                    self.emitted += 1

            def use(self):
                self.topup()
                i = self.next_use
                self.next_use += 1
                assert i < self.emitted, (i, self.emitted, self.released)
                dims, _ = loads[i]
                s = i % 4
                view = ring[s][:, 0:dims[0] * dims[1]].rearrange("p (a b) -> p a b", b=dims[1])
                return view, f"w{s}"

            def release(self, n=1):
                self.released += n
                self.topup()

        W = WStream()

        S.dma("sp", "cvs", lambda h: [h.dma_start(out=cvec, in_=cvec_d[:, :])], writes=["cvec"])
        def deferred_loads():
            for hf in range(2):
                S.dma("pool", f"xo{hf}", lambda h, hf=hf: [h.dma_start(
                    out=xTo[:, :, hf * 512:(hf + 1) * 512], in_=xT_v[:, :, HALO + hf * 512: HALO + (hf + 1) * 512])],
                    writes=[f"xTo{hf}"])
            S.dma("pool", "cst0", lambda h: [h.dma_start(out=masks, in_=masks_d[:, :])], writes=["masks"])
            S.dma("pool", "cst1", lambda h: [h.dma_start(out=ident, in_=ident_d[:, :])], writes=["ident"])
            S.dma("pool", "cst2", lambda h: [h.dma_start(
                out=wpool_sb, in_=w_pool.rearrange("g (kc p) o -> p g kc o", p=128))], writes=["wpool"])
        S.op("dve", lambda h: h.memset(ones_bf, 1.0), writes=["ones_bf"])
        S.op("dve", lambda h: h.memset(ones_f, 1.0), writes=["ones_f"])

        B = Bump(92672)
        xh = [B.take((16, 512), BF16) for _ in range(2)]
        cosT = B.take((WIN,), F32)
        sinT = B.take((WIN,), F32)
        KT = [B.take((WIN,), BF16) for _ in range(2)]
        Qz = [B.take((8, 2, 128), BF16) for _ in range(2)]
        Vaug = B.take((NKT, 2, 129), BF16)
        rt = [B.take((512,), F32) for _ in range(4)]
        EP = B.take((4096,), BF16)
        Ebuf = [EP[:, k_ * 512:(k_ + 1) * 512] for k_ in range(4)]
        PTb = [EP[:, 2048 + k_ * 512:2048 + (k_ + 1) * 512] for k_ in range(4)]
        Onb = [B.take((128,), BF16) for _ in range(4)]
        rLb = [B.take((8,), F32) for _ in range(4)]
        posi = B.take((512,), I32)
        st_f = [EP[:, k_ * 1024:(k_ + 1) * 1024].bitcast(F32) for k_ in range(4)]
        st_i = B.take((512,), I32)
        assert B.o <= ARENA_ELEMS * 2, B.o

        for s_ in range(2):
            S.op("dve", lambda h, s_=s_: h.memset(Qz[s_].rearrange("p a b c -> p (a b c)"), 0.0), writes=[f"Qz{s_}"])
        for hh_ in range(2):
            S.op("dve", lambda h, hh_=hh_: h.tensor_copy(out=Vaug[:, :, hh_, 128], in_=cvec[:, CV_VALID:CV_VALID + NKT]),
                 reads=["cvec"], writes=[f"Vval{hh_}"])

        invf = cvec[:, CV_INVF:CV_INVF + 1]
        for tc in range(6):
            sl = slice(tc * 512, (tc + 1) * 512)
            S.dma("sp", "misc", lambda h, tc=tc: [h.dma_start(
                out=posi, in_=pos[0:1, tc * 512:(tc + 1) * 512].partition_broadcast(128))], writes=["posi"])
            pf, ang, tq, kf = st_f
            S.op("dve", lambda h: h.tensor_copy(out=pf, in_=posi), reads=["posi"], writes=["pf"])
            S.op("dve", lambda h: h.tensor_scalar(out=ang, in0=pf, scalar1=invf, scalar2=None, op0=ALU.mult),
                 reads=["pf", "cvec"], writes=["ang"])
            S.op("dve", lambda h: h.tensor_scalar(out=tq, in0=ang, scalar1=float(1.0 / (2.0 * np.pi)), scalar2=None,
                                                  op0=ALU.mult), reads=["ang"], writes=["tq"])
            S.op("dve", lambda h: h.tensor_copy(out=st_i, in_=tq), reads=["tq"], writes=["ki"])
            S.op("dve", lambda h: h.tensor_copy(out=kf, in_=st_i), reads=["ki"], writes=["kf"])
            S.op("dve", lambda h: h.scalar_tensor_tensor(out=tq, in0=kf, scalar=-TWO_PI_HI, in1=ang,
                                                         op0=ALU.mult, op1=ALU.add), reads=["kf", "ang"], writes=["tq"])
            S.op("dve", lambda h: h.scalar_tensor_tensor(out=pf, in0=kf, scalar=-TWO_PI_LO, in1=tq,
                                                         op0=ALU.mult, op1=ALU.add), reads=["kf", "tq"], writes=["pf"])
            S.op("dve", lambda h: h.tensor_scalar(out=pf, in0=pf, scalar1=-3.14159, scalar2=3.14159,
                                                  op0=ALU.max, op1=ALU.min), reads=["pf"], writes=["pf"])
            S.op("act", lambda h, sl=sl: h.activation(out=sinT[:, sl], in_=pf, func=AF.Sin),
                 reads=["pf"], writes=[f"sin{tc}"])
            S.op("act", lambda h: h.activation(out=ang, in_=pf, func=AF.Abs),
                 reads=["pf"], writes=["ang"])
            S.op("act", lambda h, sl=sl: h.activation(out=cosT[:, sl], in_=ang, func=AF.Sin, scale=-1.0,
                                                      bias=float(np.pi / 2)), reads=["ang"], writes=[f"cos{tc}"])

        ps_bf3 = psb[3][:, :].bitcast(BF16)
        SCL = float(128 ** -0.5)

        def rope(tc, pa, pb, dstA, dstB, tokA, tokB, qchunk=None):
            sl = slice(tc * 512, (tc + 1) * 512)
            C, Sn = cosT[:, sl], sinT[:, sl]
            ta, tb, tcc, td = rt
            S.op("dve", lambda h: h.tensor_tensor(out=ta, in0=pa[0], in1=C, op=ALU.mult),
                 reads=[pa[1], f"cos{tc}"], writes=["rt0"])
            S.op("dve", lambda h: h.tensor_tensor(out=tb, in0=pb[0], in1=Sn, op=ALU.mult),
                 reads=[pb[1], f"sin{tc}"], writes=["rt1"])
            S.op("dve", lambda h: h.tensor_tensor(out=tcc, in0=pb[0], in1=C, op=ALU.mult),
                 reads=[pb[1], f"cos{tc}"], writes=["rt2"])
            S.op("dve", lambda h: h.tensor_tensor(out=td, in0=pa[0], in1=Sn, op=ALU.mult),
                 reads=[pa[1], f"sin{tc}"], writes=["rt3"])
            if qchunk is None:
                S.op("dve", lambda h: h.tensor_tensor(out=dstA, in0=ta, in1=tb, op=ALU.subtract),
                     reads=["rt0", "rt1"], writes=[tokA])
                S.op("dve", lambda h: h.tensor_tensor(out=dstB, in0=tcc, in1=td, op=ALU.add),
                     reads=["rt2", "rt3"], writes=[tokB])
            else:
                for s_, (x0, x1, op_, rds) in enumerate(((ta, tb, ALU.subtract, ["rt0", "rt1"]),
                                                         (tcc, td, ALU.add, ["rt2", "rt3"]))):
                    for hh_ in range(2):
                        pp = slice(64 * hh_, 64 * hh_ + 64)
                        S.op("dve", lambda h, s_=s_, x0=x0, x1=x1, op_=op_, pp=pp, hh_=hh_: h.tensor_tensor(
                            out=Qz[s_][pp, 4 * qchunk:4 * qchunk + 4, hh_, :],
                            in0=x0[pp, :].rearrange("p (a b) -> p a b", b=128),
                            in1=x1[pp, :].rearrange("p (a b) -> p a b", b=128), op=op_),
                            reads=rds + [f"Qz{s_}"], writes=[f"Q{qchunk}"])

        def proj_group(bank, wv, wtok, col0, xsrc, xtok, ncols=128, ntok=512):
            def fn(h):
                ins = None
                for dc in range(16):
                    ins = h.matmul(psb[bank][:, 0:ntok], lhsT=wv[:, dc, col0:col0 + 128], rhs=xsrc[:, dc, :],
                                   start=(dc == 0), stop=(dc == 15))
                return ins
            S.op("pe", fn, reads=[wtok] + list(xtok), writes=[f"ps{bank}"])

        def emit_xh(n, after=()):
            if n >= 16:
                return
            tcn = n % 4
            sn = n % 2
            S.dma("pool", f"xh{sn}", lambda h, sn=sn, tcn=tcn: [h.dma_start(
                out=xh[sn], in_=xT_v[:, :, tcn * 512:(tcn + 1) * 512])], reads=list(after), writes=[f"xh{sn}"])
        emit_xh(0)
        W.topup()
        emit_xh(1)
        pbank = 0
        att_units = []
        for g in range(4):
            wk, wk_t = W.use()
            wv_, wv_t = W.use()
            wq, wq_t = W.use()
            for tc in range(6):
                if tc < 4:
                    s = (4 * g + tc) % 2
                    xsrc, xtok = xh[s], [f"xh{s}"]
                else:
                    hf = tc - 4
                    xsrc, xtok = xTo[:, :, hf * 512:(hf + 1) * 512], [f"xTo{hf}"]
                ba, bb = pbank, pbank + 1
                pbank = (pbank + 2) % 4
                proj_group(ba, wk, wk_t, 0, xsrc, xtok)
                proj_group(bb, wk, wk_t, 128, xsrc, xtok)
                sl = slice(tc * 512, (tc + 1) * 512)
                rope(tc, (psb[ba][:, :], f"ps{ba}"), (psb[bb][:, :], f"ps{bb}"), KT[0][:, sl], KT[1][:, sl],
                     f"KA{tc}", f"KB{tc}")
                if tc >= 4:
                    ba, bb = pbank, pbank + 1
                    pbank = (pbank + 2) % 4
                    proj_group(ba, wq, wq_t, 0, xsrc, xtok)
                    proj_group(bb, wq, wq_t, 128, xsrc, xtok)
                    rope(tc, (psb[ba][:, :], f"ps{ba}"), (psb[bb][:, :], f"ps{bb}"), None, None, None, None,
                         qchunk=tc - 4)
                for tt in range(4):
                    kt = tc * 4 + tt
                    vb = 4 + (kt % 2)

                    def vfn(h, tt=tt, vb=vb, xsrc=xsrc, wv_=wv_):
                        ins = None
                        for dc in range(16):
                            ins = h.matmul(psb[vb][:, 0:256], lhsT=xsrc[:, dc, tt * 128:(tt + 1) * 128],
                                           rhs=wv_[:, dc, 0:256], start=(dc == 0), stop=(dc == 15))
                        return ins
                    S.op("pe", vfn, reads=[wv_t] + xtok, writes=[f"ps{vb}"])
                    S.op("act", lambda h, kt=kt, vb=vb: h.activation(
                        out=Vaug[:, kt, :, 0:128], in_=psb[vb][:, 0:256].rearrange("p (a b) -> p a b", b=128),
                        func=AF.Copy), reads=[f"ps{vb}"], writes=[f"V{kt}"])
                if tc < 4:
                    emit_xh(4 * g + tc + 2)
                    if g == 0 and tc == 1:
                        deferred_loads()

            W.release(3)
            units = [(qb, grp) for qb in range(8) for grp in range(9)]
            sbank_i = [0]
            ep_i = [0]
            state = {}

            def emit_qk(idx):
                qb, grp = units[idx]
                b = 16 + qb
                d0 = 2 * grp
                n = 2 if grp < 8 else 1
                bank = sbank_i[0] % 4
                sbank_i[0] += 1
                kts = [b - (d0 + i) for i in range(n)]
                rd = []
                for kt in kts:
                    rd += [f"KA{kt // 4}", f"KB{kt // 4}"]
                rd += [f"Q{qb // 4}"]
                def fn(h, kts=kts, bank=bank, qb=qb):
                    ins = None
                    for i, kt in enumerate(kts):
                        for s_ in range(2):
                            ins = h.matmul(psb[bank][:, i * 256:(i + 1) * 256],
                                           lhsT=KT[s_][:, kt * 128:(kt + 1) * 128],
                                           rhs=Qz[s_][:, qb, :, :].rearrange("p a b -> p (a b)"),
                                           start=(s_ == 0), stop=(s_ == 1))
                    return ins
                S.op("pe", fn, reads=list(dict.fromkeys(rd)), writes=[f"ps{bank}"])
                e = ep_i[0] % 4
                ep_i[0] += 1
                w = 2 * n * 128
                S.op("act", lambda h, bank=bank, w=w, e=e, n=n: h.activation(
                    out=Ebuf[e][:, 0:w].rearrange("p (a i q) -> p i a q", a=2, i=n),
                    in_=psb[bank][:, 0:w].rearrange("p (i a q) -> p i a q", a=2, i=n), func=AF.Exp, scale=SCL),
                    reads=[f"ps{bank}"], writes=[f"E{e}"])
                for hh_ in range(2):
                    S.op("dve", lambda h, n=n, e=e, d0=d0, hh_=hh_: h.tensor_tensor(
                        out=PTb[e][:, hh_ * n * 128:(hh_ + 1) * n * 128],
                        in0=Ebuf[e][:, hh_ * n * 128:(hh_ + 1) * n * 128],
                        in1=masks[:, d0 * 128:(d0 + n) * 128], op=ALU.mult),
                        reads=[f"E{e}", "masks"], writes=[f"PT{e}_{hh_}"])
                state[idx] = (e, kts, n)

            def emit_pv(idx):
                qb, grp = units[idx]
                e, kts, n = state.pop(idx)
                first = grp == 0
                last = grp == 8
                for hh_ in range(2):
                    ob = 4 + 2 * (qb % 2) + hh_

                    def fn(h, kts=kts, e=e, ob=ob, hh_=hh_, n=n):
                        ins = None
                        for i, kt in enumerate(kts):
                            ins = h.matmul(psb[ob][:, 0:129], lhsT=PTb[e][:, (hh_ * n + i) * 128:(hh_ * n + i + 1) * 128],
                                           rhs=Vaug[:, kt, hh_, :], start=(first and i == 0),
                                           stop=(last and i == len(kts) - 1))
                        return ins
                    S.op("pe", fn, reads=[f"PT{e}_{hh_}", f"Vval{hh_}"] + [f"V{kt}" for kt in kts],
                         writes=[f"ps{ob}"])
                    if last:
                        o = 2 * (qb % 2) + hh_
                        head = 2 * g + hh_
                        S.op("dve", lambda h, ob=ob, o=o: h.reciprocal(out=rLb[o][:, 0:1], in_=psb[ob][:, 128:129]),
                             reads=[f"ps{ob}"], writes=[f"rL{o}"])
                        S.op("dve", lambda h, ob=ob, o=o: h.tensor_scalar(
                            out=Onb[o], in0=psb[ob][:, 0:128], scalar1=rLb[o][:, 0:1], scalar2=None, op0=ALU.mult),
                            reads=[f"ps{ob}", f"rL{o}"], writes=[f"On{o}"])
                        tview = psb[ob][:, :].bitcast(BF16)[:, 512:640]

                        def late_fn(o=o, tview=tview, head=head, qb=qb, ob=ob):
                            S.op("pe", lambda h: h.transpose(out=tview, in_=Onb[o], identity=ident),
                                 reads=[f"On{o}", "ident"], writes=[f"ps{ob}"])
                            S.op("act", lambda h: h.activation(
                                out=OT[:, head, qb * 128:(qb + 1) * 128], in_=tview,
                                func=AF.Copy), reads=[f"ps{ob}"], writes=[f"OT{head}_{qb}"])
                        late.append((idx + 4, late_fn))

            ADEPTH = 3
            late = []

            def flush_late(now):
                while late and late[0][0] <= now:
                    late.pop(0)[1]()
            for idx in range(len(units)):
                emit_qk(idx)
                if idx >= ADEPTH:
                    emit_pv(idx - ADEPTH)
                    flush_late(idx - ADEPTH)
            for idx in range(max(0, len(units) - ADEPTH), len(units)):
                emit_pv(idx)
            flush_late(10 ** 9)

        if debug:
            S.dma("sp", "dbg", lambda h: [h.dma_start(out=dbg["OT"][:, :], in_=OT.rearrange("p a b -> p (a b)"))],
                  reads=[f"OT{hd}_{qb}" for hd in range(8) for qb in range(8)], writes=["dbgOT"])
        S.barrier()

        Cb = Bump(92672)
        mergedT = Cb.take((16, TOK), BF16)
        xh16 = Cb.take((16, 16), BF16)
        ubuf = [Cb.take((1040,), F32) for _ in range(3)]
        poolT = Cb.take((8, TOK), BF16)
        ypT = Cb.take((8, TOK), BF16)
        dtmp = [[Cb.take((512,), F32) for _ in range(4)] for _ in range(2)]
        t16 = Cb.take((16,), F32)
        assert Cb.o <= ARENA_ELEMS * 2

        S.dma("pool", "xh16", lambda h: [h.dma_start(out=xh16, in_=xT_v[:, :, HALO - 16:HALO])], writes=["xh16"])
        xo_tok = ["xTo0", "xTo1"]
        for j in range(8):
            if j % 2 == 0:
                wu, wu_t = W.use()
            c0 = (j % 2) * 128
            g = j // 2
            wwin = 2 ** (g + 1)
            for hf in range(2):
                proj_group(hf, wu, wu_t, c0, xTo[:, :, hf * 512:(hf + 1) * 512], [])

            def hfn(h, wu=wu, c0=c0):
                ins = None
                for dc in range(16):
                    ins = h.matmul(psb[2][:, 0:16], lhsT=wu[:, dc, c0:c0 + 128], rhs=xh16[:, dc, :],
                                   start=(dc == 0), stop=(dc == 15))
                return ins
            S.op("pe", hfn, reads=[wu_t, "xh16"], writes=["ps2"])
            u = ubuf[0]
            S.op("act", lambda h, u=u: h.activation(out=u[:, 16:528], in_=psb[0][:, :], func=AF.Copy),
                 reads=["ps0"], writes=["u_a"])
            S.op("act", lambda h, u=u: h.activation(out=u[:, 528:1040], in_=psb[1][:, :], func=AF.Copy),
                 reads=["ps1"], writes=["u_b"])
            S.op("act", lambda h, u=u: h.activation(out=u[:, 0:16], in_=psb[2][:, 0:16], func=AF.Copy),
                 reads=["ps2"], writes=["u_c"])
            src, srct = u, ["u_a", "u_b", "u_c"]
            for k in range(1, g + 2):
                dst = ubuf[1 + (k % 2)]
                dtok = f"ub{1 + (k % 2)}"
                t0 = 2 ** k - 1
                sh = 2 ** (k - 1)
                S.op("dve", lambda h, dst=dst, src=src, t0=t0, sh=sh: h.tensor_tensor(
                    out=dst[:, t0:1040], in0=src[:, t0:1040], in1=src[:, t0 - sh:1040 - sh], op=ALU.add),
                    reads=srct, writes=[dtok])
                src, srct = dst, [dtok]
            S.op("dve", lambda h, src=src, u=u, j=j, wwin=wwin: h.scalar_tensor_tensor(
                out=poolT[:, j, 16:TOK], in0=src[:, 32:1040], scalar=float(1.0 / wwin), in1=u[:, 32:1040],
                op0=ALU.mult, op1=ALU.subtract), reads=srct + ["u_a", "u_b"], writes=[f"pl{j}"])
            S.op("dve", lambda h, src=src, g=g: h.tensor_tensor(
                out=t16, in0=src[:, 16:32], in1=cvec[:, CV_INVC + g * 16:CV_INVC + (g + 1) * 16], op=ALU.mult),
                reads=srct, writes=["t16"])
            S.op("dve", lambda h, u=u, j=j: h.tensor_tensor(
                out=poolT[:, j, 0:16], in0=t16, in1=u[:, 16:32], op=ALU.subtract),
                reads=["t16", "u_a"], writes=[f"plh{j}"])
            if j % 2 == 1:
                W.release(1)
        for g in range(4):
            for oc in range(2):
                for hf in range(2):
                    bank = 4 + (g * 4 + oc * 2 + hf) % 2

                    def pfn(h, g=g, oc=oc, hf=hf, bank=bank):
                        ins = None
                        for kc in range(2):
                            ins = h.matmul(psb[bank][:, :], lhsT=wpool_sb[:, g, kc, oc * 128:(oc + 1) * 128],
                                           rhs=poolT[:, 2 * g + kc, hf * 512:(hf + 1) * 512],
                                           start=(kc == 0), stop=(kc == 1))
                        return ins
                    S.op("pe", pfn, reads=[f"pl{2 * g}", f"pl{2 * g + 1}", f"plh{2 * g}", f"plh{2 * g + 1}"],
                         writes=[f"ps{bank}"])
                    jo = 2 * g + oc
                    S.op("act", lambda h, jo=jo, hf=hf, bank=bank: h.activation(
                        out=ypT[:, jo, hf * 512:(hf + 1) * 512], in_=psb[bank][:, :], func=AF.Copy,
                        scale=cvec[:, CV_PSC + jo:CV_PSC + jo + 1]), reads=[f"ps{bank}"], writes=[f"yp{jo}_{hf}"])
        if debug:
            S.dma("sp", "dbg", lambda h: [h.dma_start(out=dbg["yp"][:, :], in_=ypT.rearrange("p a b -> p (a b)"))],
                  reads=[f"yp{jo}_{hf}" for jo in range(8) for hf in range(2)], writes=["dbgyp"])

        for j in range(16):
            wg, wg_t = W.use()
            wb, wb_t = W.use()
            for hf in range(2):
                bs = 4 * hf
                tsl = slice(hf * 512, (hf + 1) * 512)
                proj_group(bs + 0, wg, wg_t, 0, xTo[:, :, tsl], [])
                proj_group(bs + 1, wg, wg_t, 128, xTo[:, :, tsl], [])

                def yfn(h, bank, base, src, wb=wb, tsl=tsl):
                    ins = None
                    for c in range(8):
                        ins = h.matmul(psb[bank][:, :], lhsT=wb[:, base + c, 0:128], rhs=src[:, c, tsl],
                                       start=(c == 0), stop=(c == 7))
                    return ins
                S.op("pe", lambda h, bs=bs, yfn=yfn: yfn(h, bs + 2, 0, OT), reads=[wb_t, "OTreg"],
                     writes=[f"ps{bs + 2}"])
                S.op("pe", lambda h, bs=bs, yfn=yfn: yfn(h, bs + 3, 8, ypT),
                     reads=[wb_t] + [f"yp{jo}_{hf}" for jo in range(8)], writes=[f"ps{bs + 3}"])
                sA, sB, m1, m2 = dtmp[hf]
                S.op("act", lambda h, bs=bs, sA=sA: h.activation(out=sA, in_=psb[bs][:, :], func=AF.Sigmoid),
                     reads=[f"ps{bs}"], writes=[f"sA{hf}"])
                S.op("act", lambda h, bs=bs, sB=sB: h.activation(out=sB, in_=psb[bs + 1][:, :], func=AF.Sigmoid),
                     reads=[f"ps{bs + 1}"], writes=[f"sB{hf}"])
                S.op("dve", lambda h, bs=bs, sA=sA, m1=m1: h.tensor_tensor(out=m1, in0=sA, in1=psb[bs + 2][:, :],
                                                                          op=ALU.mult),
                     reads=[f"sA{hf}", f"ps{bs + 2}"], writes=[f"m1{hf}"])
                S.op("dve", lambda h, bs=bs, sB=sB, m2=m2: h.tensor_tensor(out=m2, in0=sB, in1=psb[bs + 3][:, :],
                                                                          op=ALU.mult),
                     reads=[f"sB{hf}", f"ps{bs + 3}"], writes=[f"m2{hf}"])
                S.op("dve", lambda h, m1=m1, m2=m2, j=j, tsl=tsl: h.tensor_tensor(
                    out=mergedT[:, j, tsl], in0=m1, in1=m2, op=ALU.add),
                    reads=[f"m1{hf}", f"m2{hf}"], writes=[f"mg{j}_{hf}"])
            W.release(2)
        if debug:
            S.dma("sp", "dbg", lambda h: [h.dma_start(out=dbg["mg"][:, :], in_=mergedT.rearrange("p a b -> p (a b)"))],
                  reads=[f"mg{j}_{hf}" for j in range(16) for hf in range(2)], writes=["dbgmg"])

        x1f = carve(125440, (16, TOK), F32)
        x1b = carve(43520, (16, TOK), BF16)
        Eb = Bump(76288)
        xf = [Eb.take((512,), F32) for _ in range(4)]
        sqb = [Eb.take((512,), F32) for _ in range(2)]
        SS = [[None, None], [None, None]]
        SS[0][0] = Eb.take((512,), F32)
        SS[0][1] = Eb.take((512,), F32)
        assert Eb.o <= 92672
        Ub = Bump(190976)
        lnA = Ub.take((TOK,), F32)
        lnB = Ub.take((TOK,), F32)
        lt = [Ub.take((512,), F32) for _ in range(2)]
        SS[1][0] = Ub.take((512,), F32)
        SS[1][1] = Ub.take((512,), F32)
        assert Ub.o <= ARENA_ELEMS * 2

        def stats_accum(hf, j, zap, ztok):
            s1, s2 = SS[hf]
            t1, t2 = f"S1_{hf}", f"S2_{hf}"
            if j == 0:
                S.op("act", lambda h: h.activation(out=s1, in_=zap, func=AF.Copy), reads=[ztok], writes=[t1])
                S.op("act", lambda h: h.activation(out=s2, in_=zap, func=AF.Square), reads=[ztok], writes=[t2])
            else:
                q = j % 2
                S.op("dve", lambda h: h.tensor_tensor(out=s1, in0=s1, in1=zap, op=ALU.add),
                     reads=[ztok, t1], writes=[t1])
                S.op("act", lambda h: h.activation(out=sqb[q], in_=zap, func=AF.Square),
                     reads=[ztok], writes=[f"sq{q}"])
                S.op("dve", lambda h: h.tensor_tensor(out=s2, in0=s2, in1=sqb[q], op=ALU.add),
                     reads=[f"sq{q}", t2], writes=[t2])

        def stats_final(hf, b1, b2):
            s1, s2 = SS[hf]
            S.op("pe", lambda h: h.matmul(psb[b1][:, :], lhsT=ones_f, rhs=s1, start=True, stop=True),
                 reads=[f"S1_{hf}", "ones_f"], writes=[f"ps{b1}"])
            S.op("pe", lambda h: h.matmul(psb[b2][:, :], lhsT=ones_f, rhs=s2, start=True, stop=True),
                 reads=[f"S2_{hf}", "ones_f"], writes=[f"ps{b2}"])

        def ln_tables(hf, bsum, bsq):
            tsl = slice(hf * 512, (hf + 1) * 512)
            mean, ex2 = lt
            S.op("act", lambda h: h.activation(out=mean, in_=psb[bsum][:, :], func=AF.Copy, scale=float(1.0 / D)),
                 reads=[f"ps{bsum}"], writes=["lt0"])
            S.op("act", lambda h: h.activation(out=ex2, in_=psb[bsq][:, :], func=AF.Copy, scale=float(1.0 / D)),
                 reads=[f"ps{bsq}"], writes=["lt1"])
            S.op("dve", lambda h: h.tensor_tensor(out=lnB[:, tsl], in0=mean, in1=mean, op=ALU.mult),
                 reads=["lt0"], writes=[f"lnB{hf}"])
            S.op("dve", lambda h: h.tensor_tensor(out=ex2, in0=ex2, in1=lnB[:, tsl], op=ALU.subtract),
                 reads=["lt1", f"lnB{hf}"], writes=["lt1"])
            S.op("dve", lambda h: h.tensor_scalar(out=ex2, in0=ex2, scalar1=float(LN_EPS), scalar2=None,
                                                  op0=ALU.add), reads=["lt1"], writes=["lt1"])
            S.op("act", lambda h: h.activation(out=ex2, in_=ex2, func=AF.Sqrt), reads=["lt1"], writes=["lt1"])
            S.op("dve", lambda h: h.reciprocal(out=lnA[:, tsl], in_=ex2), reads=["lt1"], writes=[f"lnA{hf}"])
            S.op("dve", lambda h: h.scalar_tensor_tensor(out=lnB[:, tsl], in0=mean, scalar=-1.0, in1=lnA[:, tsl],
                                                         op0=ALU.mult, op1=ALU.mult),
                 reads=["lt0", f"lnA{hf}"], writes=[f"lnB{hf}"])

        def ln_apply(j, hf, buf, ztok, gcol, bcol, out_tok, bf_out=None):
            tsl = slice(hf * 512, (hf + 1) * 512)
            zz = buf[:, j, tsl]
            g_ap = cvec[:, gcol + j:gcol + j + 1]
            b_ap = cvec[:, bcol + j:bcol + j + 1]
            S.op("dve", lambda h: h.tensor_tensor(out=zz, in0=zz, in1=lnA[:, tsl], op=ALU.mult),
                 reads=[ztok, f"lnA{hf}"], writes=[ztok])
            S.op("dve", lambda h: h.tensor_tensor(out=zz, in0=zz, in1=lnB[:, tsl], op=ALU.add),
                 reads=[ztok, f"lnB{hf}"], writes=[ztok])
            if bf_out is not None:
                S.op("act", lambda h: h.activation(out=bf_out[:, j, tsl], in_=zz, func=AF.Identity,
                                                   scale=g_ap, bias=b_ap),
                     reads=[ztok], writes=[out_tok])
            S.op("act", lambda h: h.activation(out=zz, in_=zz, func=AF.Identity, scale=g_ap, bias=b_ap),
                 reads=[ztok] + ([out_tok] if bf_out is not None else []), writes=[ztok])

        def stats_mm(bank, src, srctok, first, last):
            S.op("pe", lambda h: h.matmul(psb[bank][:, :], lhsT=ones_f, rhs=src, start=first, stop=last),
                 reads=[srctok], writes=[f"ps{bank}"])

        import collections
        pending = collections.deque()

        def drain(n):
            for _ in range(n):
                if pending:
                    pending.popleft()()

        e_iters = [(hf, jp, jj) for hf in range(2) for jp in range(8) for jj in range(2)]

        def ln1_sched(hf):
            stats_final(hf, 4 + hf, 6 + hf)
            pending.append(lambda: ln_tables(hf, 4 + hf, 6 + hf))
            for jx in range(16):
                pending.append(lambda jx=jx: ln_apply(jx, hf, x1f, f"z{jx}_{hf}", CV_LN1G, CV_LN1B,
                                                      f"x1b{jx}_{hf}", bf_out=x1b))

        def emit_xf(i):
            hf, jp, jj = e_iters[i]
            j = 2 * jp + jj
            s = i % 4
            S.dma("sp", f"xf{s}", lambda h, s=s, j=j, hf=hf: [h.dma_start(
                out=xf[s], in_=xT[j * 128:(j + 1) * 128, HALO + hf * 512: HALO + (hf + 1) * 512])],
                writes=[f"xf{s}"] + (["OTreg"] if i < 4 else []))
        emit_xf(0)
        emit_xf(1)
        wo = wo_t = None
        for i, (hf, jp, jj) in enumerate(e_iters):
            if jj == 0:
                wo, wo_t = W.use()
            if i + 2 < len(e_iters):
                emit_xf(i + 2)
            j = 2 * jp + jj
            tsl = slice(hf * 512, (hf + 1) * 512)
            bank = i % 4

            def mfn(h, wo=wo, jj=jj, tsl=tsl, bank=bank):
                ins = None
                for kc in range(16):
                    ins = h.matmul(psb[bank][:, :], lhsT=wo[:, kc, jj * 128:(jj + 1) * 128], rhs=mergedT[:, kc, tsl],
                                   start=(kc == 0), stop=(kc == 15))
                return ins
            S.op("pe", mfn, reads=[wo_t] + [f"mg{kc}_{hf}" for kc in range(16)], writes=[f"ps{bank}"])
            s_ = i % 4
            ztok = f"z{j}_{hf}"
            S.op("dve", lambda h, s_=s_, j=j, tsl=tsl, bank=bank: h.scalar_tensor_tensor(
                out=x1f[:, j, tsl], in0=xf[s_], scalar=ALPHA, in1=psb[bank][:, :], op0=ALU.mult, op1=ALU.add),
                reads=[f"xf{s_}", f"ps{bank}"], writes=[ztok])
            stats_accum(hf, j, x1f[:, j, tsl], ztok)
            if jj == 1:
                W.release(1)
            drain(1)
            if i == 17:
                ln1_sched(0)
                drain(1)
        drain(len(pending))
        if debug:
            S.dma("sp", "dbg", lambda h: [h.dma_start(out=dbg["x1"][:, :], in_=x1f.rearrange("p a b -> p (a b)"))],
                  reads=[f"z{j}_{hf}" for j in range(16) for hf in range(2)], writes=["dbgx1"])

        hTg = [carve(92672 + b_ * 16384, (8, TOK), BF16) for b_ in range(2)]
        Fb = Bump(76288)
        rb = [Fb.take((512,), F32) for _ in range(2)]
        assert Fb.o <= 84480
        fbank = [0]
        rcnt = [0]

        def nextbank():
            b_ = fbank[0] % 4
            fbank[0] += 1
            return b_

        def f1_one(hg, i, cc, th, w1s, w1_t, ndrain):
            buf = hTg[hg % 2]
            hc = 2 * i + cc
            bank = nextbank()
            tsl = slice(th * 512, (th + 1) * 512)

            def f1(h):
                ins = None
                for kc in range(16):
                    ins = h.matmul(psb[bank][:, :], lhsT=w1s[:, kc, cc * 128:(cc + 1) * 128],
                                   rhs=x1b[:, kc, tsl], start=(kc == 0), stop=(kc == 15))
                return ins
            S.op("pe", f1, reads=[w1_t] + [f"x1b{jx}_{th}" for jx in range(16)], writes=[f"ps{bank}"])
            r = rcnt[0] % 2
            rcnt[0] += 1
            S.op("act", lambda h: h.activation(out=rb[r], in_=psb[bank][:, :], func=AF.Relu),
                 reads=[f"ps{bank}"], writes=[f"r{r}"])
            S.op("dve", lambda h: h.tensor_tensor(out=buf[:, hc, tsl], in0=rb[r], in1=rb[r], op=ALU.mult),
                 reads=[f"r{r}"], writes=[f"h{hg % 2}_{hc}_{th}"])
            drain(ndrain)

        f1_count = [0]

        def emit_F1_half(hg, th):
            for i in range(4):
                w1s, w1_t = W.use()
                for cc in range(2):
                    f1_one(hg, i, cc, th, w1s, w1_t, 1)
                    f1_count[0] += 1
                    if f1_count[0] == 2:
                        ln1_sched(1)
                        drain(1)
                W.release(1)

        def emit_F1(hg):
            if hg == 0:
                raise AssertionError
            else:
                for i in range(4):
                    w1s, w1_t = W.use()
                    for cc in range(2):
                        for th in range(2):
                            f1_one(hg, i, cc, th, w1s, w1_t, 1)
                    W.release(1)

        def f2_group(hg, j, jj, th, w2s, w2_t):
            buf = hTg[hg % 2]
            bank = nextbank()
            tsl = slice(th * 512, (th + 1) * 512)

            def f2(h):
                ins = None
                for hc in range(8):
                    ins = h.matmul(psb[bank][:, :], lhsT=w2s[:, hc, jj * 128:(jj + 1) * 128],
                                   rhs=buf[:, hc, tsl], start=(hc == 0), stop=(hc == 7))
                return ins
            S.op("pe", f2, reads=[w2_t] + [f"h{hg % 2}_{hc}_{th}" for hc in range(8)], writes=[f"ps{bank}"])
            ztok = f"z{j}_{th}"
            if hg == 0:
                S.op("dve", lambda h: h.scalar_tensor_tensor(
                    out=x1f[:, j, tsl], in0=x1f[:, j, tsl], scalar=ALPHA, in1=psb[bank][:, :],
                    op0=ALU.mult, op1=ALU.add), reads=[f"ps{bank}", ztok], writes=[ztok])
            else:
                S.op("dve", lambda h: h.tensor_tensor(
                    out=x1f[:, j, tsl], in0=x1f[:, j, tsl], in1=psb[bank][:, :], op=ALU.add),
                    reads=[f"ps{bank}", ztok], writes=[ztok])
            if hg == 7:
                stats_accum(th, j, x1f[:, j, tsl], ztok)
                drain(1)

        def ln2_sched(th):
            tsl = slice(th * 512, (th + 1) * 512)
            stats_final(th, 4 + th, 6 + th)
            pending.append(lambda: ln_tables(th, 4 + th, 6 + th))

            def ln2_apply(j):
                ln_apply(j, th, x1f, f"z{j}_{th}", CV_LN2G, CV_LN2B, None)
                if j % 4 == 3:
                    S.dma("sp", "out", lambda h: [h.dma_start(out=outT_v[:, j - 3:j + 1, tsl],
                                                              in_=x1f[:, j - 3:j + 1, tsl])],
                          reads=[f"z{jx}_{th}" for jx in range(j - 3, j + 1)], writes=[f"out{th}_{j // 4}"])
            for j in range(16):
                pending.append(lambda j=j: ln2_apply(j))
            drain(1)

        def emit_F2(hg):
            if hg < 7:
                for jp in range(8):
                    w2s, w2_t = W.use()
                    for jj in range(2):
                        for th in range(2):
                            f2_group(hg, 2 * jp + jj, jj, th, w2s, w2_t)
                    W.release(1)
            else:
                for th in range(2):
                    for jp in range(8):
                        w2s, w2_t = W.use()
                        for jj in range(2):
                            f2_group(hg, 2 * jp + jj, jj, th, w2s, w2_t)
                        W.release(1)
                        if th == 1 and jp == 0:
                            ln2_sched(0)
                ln2_sched(1)

        emit_F1_half(0, 0)
        emit_F1_half(1, 0)
        drain(len(pending))
        emit_F1_half(0, 1)
        emit_F1_half(1, 1)
        emit_F2(0)
        for hg in range(2, 8):
            emit_F1(hg)
            emit_F2(hg - 1)
        emit_F2(7)
        drain(len(pending))
        S.barrier()
        S.emit(block)
        build_nc.last_sched = S
    return nc


def _host_prep(x, positions, w_in, w_pool, pool_scale, w_branch_attn, w_branch_pool, w_out,
               ln_mix_g, ln_mix_b, w_ff1, w_ff2, ln_ff_g, ln_ff_b):
    f32 = np.float32
    x2 = np.asarray(x, f32)[0]
    xTfull = np.ascontiguousarray(x2.T)
    posf = np.asarray(positions, np.int32)[0]
    W = np.asarray(w_in, f32)[0]
    cols = []
    for g in range(4):
        a, b = 2 * g, 2 * g + 1
        for base in (0, 1024):
            A = np.concatenate([np.arange(base + a * 128, base + a * 128 + 64),
                                np.arange(base + b * 128, base + b * 128 + 64)])
            Bc = np.concatenate([np.arange(base + a * 128 + 64, base + a * 128 + 128),
                                 np.arange(base + b * 128 + 64, base + b * 128 + 128)])
            cols += [A, Bc]
        cols.append(np.arange(2048 + a * 128, 2048 + a * 128 + 256))
    w_att = np.ascontiguousarray(W[:, np.concatenate(cols)])
    w_u = np.ascontiguousarray(W[:, 3072:4096])
    gcols = []
    for j in range(16):
        gcols.append(np.arange(4096 + j * 128, 4096 + (j + 1) * 128))
        gcols.append(np.arange(6144 + j * 128, 6144 + (j + 1) * 128))
    w_g = np.ascontiguousarray(W[:, np.concatenate(gcols)])
    w_b = np.ascontiguousarray(np.concatenate([np.asarray(w_branch_attn, f32)[0],
                                               np.asarray(w_branch_pool, f32)[0]], axis=0))
    shared = dict(
        w_att=w_att, w_u=w_u, w_g=w_g, w_b=w_b,
        w_out=np.ascontiguousarray(np.asarray(w_out, f32)[0]),
        w1=np.ascontiguousarray(np.asarray(w_ff1, f32)[0]),
        w2=np.ascontiguousarray(np.asarray(w_ff2, f32)[0]),
        w_pool=np.ascontiguousarray(np.asarray(w_pool, f32)[0]),
    )
    k = np.arange(128)[:, None]
    q = np.arange(128)[None, :]
    masks = np.zeros((128, NDELTA, 128), f32)
    for dl in range(NDELTA):
        dist = 128 * dl + q - k
        m = ((dist >= 0) & (dist <= 128)).astype(f32)
        m += ((dist >= 0) & (dist <= 512) & (dist % 4 == 0)).astype(f32)
        m += ((dist >= 0) & (dist <= 2048) & (dist % 16 == 0)).astype(f32)
        masks[:, dl, :] = m
    shared["masks"] = np.ascontiguousarray(masks.reshape(128, NDELTA * 128))
    shared["ident"] = np.eye(128, dtype=f32)
    half = 64
    inv_freq = np.power(f32(10000.0), -(np.arange(half, dtype=f32) / f32(half))).astype(f32)

    def pj(v, n):
        return np.asarray(v, f32).reshape(n, 128).T

    in_maps = []
    for c in range(NCORES):
        lo = c * TOK - HALO
        xw = np.zeros((D, WIN), f32)
        pw = np.zeros((1, WIN), np.int32)
        s0 = max(lo, 0)
        xw[:, s0 - lo:] = xTfull[:, s0:(c + 1) * TOK]
        pw[0, s0 - lo:] = posf[s0:(c + 1) * TOK]
        cv = np.zeros((128, CV_N), f32)
        cv[:, CV_INVF] = np.concatenate([inv_freq, inv_freq])
        gp = lo + np.arange(NKT)[None, :] * 128 + np.arange(128)[:, None]
        cv[:, CV_VALID:CV_VALID + NKT] = (gp >= 0).astype(f32)
        for g, wwin in enumerate((2, 4, 8, 16)):
            cnt = np.minimum(c * TOK + np.arange(16) + 1, wwin).astype(f32)
            cv[:, CV_INVC + g * 16:CV_INVC + (g + 1) * 16] = (f32(1.0) / cnt)[None, :]
        cv[:, CV_PSC:CV_PSC + 8] = pj(np.asarray(pool_scale)[0], 8)
        cv[:, CV_LN1G:CV_LN1G + 16] = pj(np.asarray(ln_mix_g)[0], 16)
        cv[:, CV_LN1B:CV_LN1B + 16] = pj(np.asarray(ln_mix_b)[0], 16)
        cv[:, CV_LN2G:CV_LN2G + 16] = pj(np.asarray(ln_ff_g)[0], 16)
        cv[:, CV_LN2B:CV_LN2B + 16] = pj(np.asarray(ln_ff_b)[0], 16)
        m = dict(shared)
        m.update(xT=xw, pos=pw, cvec=cv)
        in_maps.append(m)
    return in_maps


def kernel(**inputs):
    in_maps = _host_prep(**inputs)
    nc = build_nc()
    res = run_bass_kernel_spmd(nc, in_maps, core_ids=list(range(NCORES)))
    out = np.empty((1, S_TOT, D), np.float32)
    for c in range(NCORES):
        out[0, c * TOK:(c + 1) * TOK, :] = np.asarray(res.results[c]["outT"], np.float32).T
    return out
```

```python
import contextlib
import numpy as np
import concourse.bass as bass
import concourse.mybir as mybir
from concourse.bass_utils import run_bass_kernel_spmd

F32 = mybir.dt.float32
BF16 = mybir.dt.bfloat16
I32 = mybir.dt.int32
AF = mybir.ActivationFunctionType
ALU = mybir.AluOpType

NCORES = 8
D = 2048
S_TOT = 8192
TOK = 1024
HALO = 2048
WIN = TOK + HALO
NKT = WIN // 128
DFF = 8192
ALPHA = float(2.0 ** 0.25)
LN_EPS = 1e-5
NDELTA = 17
TWO_PI_HI = 6.28125
TWO_PI_LO = 0.0019353071795864769
ARENA_ELEMS = 105472

CV_INVF = 0
CV_VALID = 1
CV_INVC = CV_VALID + NKT
CV_PSC = CV_INVC + 64
CV_LN1G = CV_PSC + 8
CV_LN1B = CV_LN1G + 16
CV_LN2G = CV_LN1B + 16
CV_LN2B = CV_LN2G + 16
CV_N = CV_LN2B + 16


class Sched:
    def __init__(self, nc, sems):
        self.nc = nc
        self.eng = {}
        for name in ("pe", "act", "dve", "pool", "sp"):
            self.eng[name] = dict(cnt=0, ops=[], known={}, log=[])
        self.lastw = {}
        self.reads = {}
        self.semobj = dict(sems)
        self.dmacnt = {}

    def _waits(self, e, reads, writes):
        evs = []
        for t in reads:
            if t in self.lastw:
                evs.append(self.lastw[t])
        for t in writes:
            if t in self.lastw:
                evs.append(self.lastw[t])
            evs.extend(self.reads.get(t, []))
        E = self.eng[e]
        need = {}
        for (sk, v) in evs:
            if E["known"].get(sk, 0) >= v:
                continue
            need[sk] = max(need.get(sk, 0), v)
        for sk, v in need.items():
            E["known"][sk] = v
        return list(need.items())

    def _record(self, ev, reads, writes):
        for t in reads:
            self.reads.setdefault(t, []).append(ev)
        for t in writes:
            self.lastw[t] = ev
            self.reads[t] = []

    def op(self, e, fn, reads=(), writes=()):
        E = self.eng[e]
        waits = self._waits(e, reads, writes)
        if e == "pe":
            waits = [(sk, v) for (sk, v) in waits if sk != "pe"]
        E["cnt"] += 1
        ev = (e, E["cnt"])
        semobj = self.semobj

        def run(h, waits=waits, fn=fn, e=e):
            for sk, v in waits:
                h.wait_ge(semobj[sk], v)
            fn(h).then_inc(semobj[e], 1)
        E["ops"].append(run)
        E["log"].append((list(waits), (e, 1)))
        self._record(ev, reads, writes)
        return ev

    def dma(self, e, semkey, fn, reads=(), writes=(), n=1):
        E = self.eng[e]
        waits = self._waits(e, reads, writes)
        self.dmacnt[semkey] = self.dmacnt.get(semkey, 0) + 16 * n
        ev = (semkey, self.dmacnt[semkey])
        semobj = self.semobj

        def run(h, waits=waits, fn=fn, semkey=semkey):
            for sk, v in waits:
                h.wait_ge(semobj[sk], v)
            for ins in fn(h):
                ins.then_inc(semobj[semkey], 16)
        E["ops"].append(run)
        E["log"].append((list(waits), (semkey, 16 * n)))
        self._record(ev, reads, writes)
        return ev

    def barrier(self):
        evs = [(e, E["cnt"]) for e, E in self.eng.items() if E["cnt"] > 0]
        evs += list(self.dmacnt.items())
        semobj = self.semobj
        for e, E in self.eng.items():
            waits = [(sk, v) for (sk, v) in evs if E["known"].get(sk, 0) < v]
            for sk, v in waits:
                E["known"][sk] = v

            def run(h, waits=waits):
                for sk, v in waits:
                    h.wait_ge(semobj[sk], v)
            E["ops"].append(run)
            E["log"].append((list(waits), None))
        self.lastw.clear()
        self.reads.clear()

    def emit(self, block):
        def mk(name):
            def body(h):
                for r in self.eng[name]["ops"]:
                    r(h)
            return body
        block.tensor(mk("pe"))
        block.scalar(mk("act"))
        block.vector(mk("dve"))
        block.gpsimd(mk("pool"))
        block.sync(mk("sp"))


def build_nc(debug=False):
    nc = bass.Bass("TRN2", target_bir_lowering=False)

    def din(name, shape, dt=F32):
        return nc.dram_tensor(name, list(shape), dt, kind="ExternalInput").ap()

    xT = din("xT", [D, WIN])
    pos = din("pos", [1, WIN], I32)
    w_att = din("w_att", [D, 3072])
    w_u = din("w_u", [D, 1024])
    w_g = din("w_g", [D, 4096])
    w_b = din("w_b", [D, 2048])
    w_out = din("w_out", [D, D])
    w1 = din("w1", [D, DFF])
    w2 = din("w2", [DFF, D])
    w_pool = din("w_pool", [4, 256, 256])
    masks_d = din("masks", [128, NDELTA * 128])
    ident_d = din("ident", [128, 128])
    cvec_d = din("cvec", [128, CV_N])
    outT = nc.dram_tensor("outT", [D, TOK], F32, kind="ExternalOutput").ap()
    dbg = {}
    if debug:
        dbg["OT"] = nc.dram_tensor("dbg_OT", [128, 8 * TOK], BF16, kind="ExternalOutput").ap()
        dbg["yp"] = nc.dram_tensor("dbg_yp", [128, 8 * TOK], BF16, kind="ExternalOutput").ap()
        dbg["mg"] = nc.dram_tensor("dbg_mg", [128, 16 * TOK], BF16, kind="ExternalOutput").ap()
        dbg["x1"] = nc.dram_tensor("dbg_x1", [128, 16 * TOK], F32, kind="ExternalOutput").ap()

    def kview(w):
        return w.rearrange("(k p) c -> p k c", p=128)

    xT_v = kview(xT)
    watt_v, wu_v, wg_v, wb_v, wout_v, w1_v, w2_v = (kview(w) for w in (w_att, w_u, w_g, w_b, w_out, w1, w2))
    outT_v = kview(outT)

    with contextlib.ExitStack() as st:
        arena = st.enter_context(nc.sbuf_tensor("arena", [128, ARENA_ELEMS], BF16))
        psb = [st.enter_context(nc.psum_tensor(f"ps{i}", [128, 512], F32)) for i in range(8)]
        semnames = ["pe", "act", "dve", "pool", "sp", "w0", "w1", "w2", "w3", "xh0", "xh1", "xo0", "xo1", "cst0", "cst1", "cst2", "cvs", "xh16", "misc",
                    "xf0", "xf1", "xf2", "xf3", "out", "dbg"]
        sems = {k: st.enter_context(nc.semaphore(k)) for k in semnames}
        block = st.enter_context(nc.Block())
        S = Sched(nc, sems)

        def carve(off, dims, dt):
            n = int(np.prod(dims))
            esz = 2 if dt == BF16 else 4
            assert off % 4 == 0 and off + n * esz <= ARENA_ELEMS * 2, (off, dims)
            ap = arena[:, off // 2: off // 2 + n * esz // 2]
            if dt != BF16:
                ap = ap.bitcast(dt)
            if len(dims) == 2:
                ap = ap.rearrange("p (a b) -> p a b", b=dims[1])
            elif len(dims) == 3:
                ap = ap.rearrange("p (a b c) -> p a b c", b=dims[1], c=dims[2])
            return ap

        class Bump:
            def __init__(self, base):
                self.o = base

            def take(self, dims, dt):
                n = int(np.prod(dims)) * (2 if dt == BF16 else 4)
                n4 = (n + 31) // 32 * 32
                ap = carve(self.o, dims, dt)
                self.o += n4
                return ap

        P = Bump(0)
        ring = [P.take((4096,), BF16) for _ in range(4)]
        masks = P.take((NDELTA * 128,), BF16)
        ident = P.take((128,), BF16)
        ones_bf = P.take((128,), BF16)
        ones_f = P.take((128,), F32)
        cvec = P.take((CV_N,), F32)
        wpool_sb = P.take((4, 2, 256), BF16)
        assert P.o <= 43520, P.o
        L1 = Bump(43520)
        xTo = L1.take((16, TOK), BF16)
        OT = L1.take((8, TOK), BF16)
        assert L1.o == 92672

        loads = []
        for g in range(4):
            for i in (1, 2, 0):
                loads.append(((16, 256), watt_v[:, :, g * 768 + i * 256: g * 768 + (i + 1) * 256]))
        for i in range(4):
            loads.append(((16, 256), wu_v[:, :, i * 256:(i + 1) * 256]))
        for j in range(16):
            loads.append(((16, 256), wg_v[:, :, j * 256:(j + 1) * 256]))
            loads.append(((16, 128), wb_v[:, :, j * 128:(j + 1) * 128]))
        for hf_ in range(2):
            for jp in range(8):
                loads.append(((16, 256), wout_v[:, :, jp * 256:(jp + 1) * 256]))
        def _w1_loads(hg):
            for i in range(4):
                c0 = hg * 1024 + i * 256
                loads.append(((16, 256), w1_v[:, :, c0:c0 + 256]))

        def _w2_loads(hg):
            for jp in range(8):
                loads.append(((8, 256), w2_v[:, hg * 8:(hg + 1) * 8, jp * 256:(jp + 1) * 256]))
        _w1_loads(0)
        _w1_loads(1)
        _w1_loads(0)
        _w1_loads(1)
        _w2_loads(0)
        for hg in range(2, 8):
            _w1_loads(hg)
            _w2_loads(hg - 1)
        _w2_loads(7)
        _w2_loads(7)

        class WStream:
            def __init__(self):
                self.emitted = 0
                self.released = 0
                self.next_use = 0

            def _emit_load(self, i, after=()):
                dims, src = loads[i]
                s = i % 4
                dst = ring[s][:, 0:dims[0] * dims[1]].rearrange("p (a b) -> p a b", b=dims[1])
                S.dma("pool", f"w{s}", lambda h, dst=dst, src=src: [h.dma_start(out=dst, in_=src)],
                      reads=list(after), writes=[f"w{s}"])

            def topup(self, cap=4):
                while self.emitted < min(self.released + cap, len(loads)):
                    self._emit_load(self.emitted)
                    self.emitted += 1

            def use(self):
                self.topup()
                i = self.next_use
                self.next_use += 1
                assert i < self.emitted, (i, self.emitted, self.released)
                dims, _ = loads[i]
                s = i % 4
                view = ring[s][:, 0:dims[0] * dims[1]].rearrange("p (a b) -> p a b", b=dims[1])
                return view, f"w{s}"

            def release(self, n=1):
                self.released += n
                self.topup()

        W = WStream()

        S.dma("sp", "cvs", lambda h: [h.dma_start(out=cvec, in_=cvec_d[:, :])], writes=["cvec"])
        def deferred_loads():
            for hf in range(2):
                S.dma("pool", f"xo{hf}", lambda h, hf=hf: [h.dma_start(
                    out=xTo[:, :, hf * 512:(hf + 1) * 512], in_=xT_v[:, :, HALO + hf * 512: HALO + (hf + 1) * 512])],
                    writes=[f"xTo{hf}"])
            S.dma("pool", "cst0", lambda h: [h.dma_start(out=masks, in_=masks_d[:, :])], writes=["masks"])
            S.dma("pool", "cst1", lambda h: [h.dma_start(out=ident, in_=ident_d[:, :])], writes=["ident"])
            S.dma("pool", "cst2", lambda h: [h.dma_start(
                out=wpool_sb, in_=w_pool.rearrange("g (kc p) o -> p g kc o", p=128))], writes=["wpool"])
        S.op("dve", lambda h: h.memset(ones_bf, 1.0), writes=["ones_bf"])
        S.op("dve", lambda h: h.memset(ones_f, 1.0), writes=["ones_f"])

        B = Bump(92672)
        xh = [B.take((16, 512), BF16) for _ in range(2)]
        cosT = B.take((WIN,), F32)
        sinT = B.take((WIN,), F32)
        KT = [B.take((WIN,), BF16) for _ in range(2)]
        Qz = [B.take((8, 2, 128), BF16) for _ in range(2)]
        Vaug = B.take((NKT, 2, 129), BF16)
        rt = [B.take((512,), F32) for _ in range(4)]
        EP = B.take((4096,), BF16)
        Ebuf = [EP[:, k_ * 512:(k_ + 1) * 512] for k_ in range(4)]
        PTb = [EP[:, 2048 + k_ * 512:2048 + (k_ + 1) * 512] for k_ in range(4)]
        Onb = [B.take((128,), BF16) for _ in range(4)]
        rLb = [B.take((8,), F32) for _ in range(4)]
        posi = B.take((512,), I32)
        st_f = [EP[:, k_ * 1024:(k_ + 1) * 1024].bitcast(F32) for k_ in range(4)]
        st_i = B.take((512,), I32)
        assert B.o <= ARENA_ELEMS * 2, B.o

        for s_ in range(2):
            S.op("dve", lambda h, s_=s_: h.memset(Qz[s_].rearrange("p a b c -> p (a b c)"), 0.0), writes=[f"Qz{s_}"])
        for hh_ in range(2):
            S.op("dve", lambda h, hh_=hh_: h.tensor_copy(out=Vaug[:, :, hh_, 128], in_=cvec[:, CV_VALID:CV_VALID + NKT]),
                 reads=["cvec"], writes=[f"Vval{hh_}"])

        invf = cvec[:, CV_INVF:CV_INVF + 1]
        for tc in range(6):
            sl = slice(tc * 512, (tc + 1) * 512)
            S.dma("sp", "misc", lambda h, tc=tc: [h.dma_start(
                out=posi, in_=pos[0:1, tc * 512:(tc + 1) * 512].partition_broadcast(128))], writes=["posi"])
            pf, ang, tq, kf = st_f
            S.op("dve", lambda h: h.tensor_copy(out=pf, in_=posi), reads=["posi"], writes=["pf"])
            S.op("dve", lambda h: h.tensor_scalar(out=ang, in0=pf, scalar1=invf, scalar2=None, op0=ALU.mult),
                 reads=["pf", "cvec"], writes=["ang"])
            S.op("dve", lambda h: h.tensor_scalar(out=tq, in0=ang, scalar1=float(1.0 / (2.0 * np.pi)), scalar2=None,
                                                  op0=ALU.mult), reads=["ang"], writes=["tq"])
            S.op("dve", lambda h: h.tensor_copy(out=st_i, in_=tq), reads=["tq"], writes=["ki"])
            S.op("dve", lambda h: h.tensor_copy(out=kf, in_=st_i), reads=["ki"], writes=["kf"])
            S.op("dve", lambda h: h.scalar_tensor_tensor(out=tq, in0=kf, scalar=-TWO_PI_HI, in1=ang,
                                                         op0=ALU.mult, op1=ALU.add), reads=["kf", "ang"], writes=["tq"])
            S.op("dve", lambda h: h.scalar_tensor_tensor(out=pf, in0=kf, scalar=-TWO_PI_LO, in1=tq,
                                                         op0=ALU.mult, op1=ALU.add), reads=["kf", "tq"], writes=["pf"])
            S.op("dve", lambda h: h.tensor_scalar(out=pf, in0=pf, scalar1=-3.14159, scalar2=3.14159,
                                                  op0=ALU.max, op1=ALU.min), reads=["pf"], writes=["pf"])
            S.op("act", lambda h, sl=sl: h.activation(out=sinT[:, sl], in_=pf, func=AF.Sin),
                 reads=["pf"], writes=[f"sin{tc}"])
            S.op("act", lambda h: h.activation(out=ang, in_=pf, func=AF.Abs),
                 reads=["pf"], writes=["ang"])
            S.op("act", lambda h, sl=sl: h.activation(out=cosT[:, sl], in_=ang, func=AF.Sin, scale=-1.0,
                                                      bias=float(np.pi / 2)), reads=["ang"], writes=[f"cos{tc}"])

        ps_bf3 = psb[3][:, :].bitcast(BF16)
        SCL = float(128 ** -0.5)

        def rope(tc, pa, pb, dstA, dstB, tokA, tokB, qchunk=None):
            sl = slice(tc * 512, (tc + 1) * 512)
            C, Sn = cosT[:, sl], sinT[:, sl]
            ta, tb, tcc, td = rt
            S.op("dve", lambda h: h.tensor_tensor(out=ta, in0=pa[0], in1=C, op=ALU.mult),
                 reads=[pa[1], f"cos{tc}"], writes=["rt0"])
            S.op("dve", lambda h: h.tensor_tensor(out=tb, in0=pb[0], in1=Sn, op=ALU.mult),
                 reads=[pb[1], f"sin{tc}"], writes=["rt1"])
            S.op("dve", lambda h: h.tensor_tensor(out=tcc, in0=pb[0], in1=C, op=ALU.mult),
                 reads=[pb[1], f"cos{tc}"], writes=["rt2"])
            S.op("dve", lambda h: h.tensor_tensor(out=td, in0=pa[0], in1=Sn, op=ALU.mult),
                 reads=[pa[1], f"sin{tc}"], writes=["rt3"])
            if qchunk is None:
                S.op("dve", lambda h: h.tensor_tensor(out=dstA, in0=ta, in1=tb, op=ALU.subtract),
                     reads=["rt0", "rt1"], writes=[tokA])
                S.op("dve", lambda h: h.tensor_tensor(out=dstB, in0=tcc, in1=td, op=ALU.add),
                     reads=["rt2", "rt3"], writes=[tokB])
            else:
                for s_, (x0, x1, op_, rds) in enumerate(((ta, tb, ALU.subtract, ["rt0", "rt1"]),
                                                         (tcc, td, ALU.add, ["rt2", "rt3"]))):
                    for hh_ in range(2):
                        pp = slice(64 * hh_, 64 * hh_ + 64)
                        S.op("dve", lambda h, s_=s_, x0=x0, x1=x1, op_=op_, pp=pp, hh_=hh_: h.tensor_tensor(
                            out=Qz[s_][pp, 4 * qchunk:4 * qchunk + 4, hh_, :],
                            in0=x0[pp, :].rearrange("p (a b) -> p a b", b=128),
                            in1=x1[pp, :].rearrange("p (a b) -> p a b", b=128), op=op_),
                            reads=rds + [f"Qz{s_}"], writes=[f"Q{qchunk}"])

        def proj_group(bank, wv, wtok, col0, xsrc, xtok, ncols=128, ntok=512):
            def fn(h):
                ins = None
                for dc in range(16):
                    ins = h.matmul(psb[bank][:, 0:ntok], lhsT=wv[:, dc, col0:col0 + 128], rhs=xsrc[:, dc, :],
                                   start=(dc == 0), stop=(dc == 15))
                return ins
            S.op("pe", fn, reads=[wtok] + list(xtok), writes=[f"ps{bank}"])

        def emit_xh(n, after=()):
            if n >= 16:
                return
            tcn = n % 4
            sn = n % 2
            S.dma("pool", f"xh{sn}", lambda h, sn=sn, tcn=tcn: [h.dma_start(
                out=xh[sn], in_=xT_v[:, :, tcn * 512:(tcn + 1) * 512])], reads=list(after), writes=[f"xh{sn}"])
        emit_xh(0)
        W._emit_load(0)
        W._emit_load(1)
        emit_xh(1)
        W._emit_load(2, after=["xh1"])
        W._emit_load(3, after=["xh1"])
        W.emitted = 4
        pbank = 0
        att_units = []
        for g in range(4):
            wk, wk_t = W.use()
            wv_, wv_t = W.use()
            wq, wq_t = W.use()
            for tc in range(6):
                if tc < 4:
                    s = (4 * g + tc) % 2
                    xsrc, xtok = xh[s], [f"xh{s}"]
                else:
                    hf = tc - 4
                    xsrc, xtok = xTo[:, :, hf * 512:(hf + 1) * 512], [f"xTo{hf}"]
                ba, bb = pbank, pbank + 1
                pbank = (pbank + 2) % 4
                proj_group(ba, wk, wk_t, 0, xsrc, xtok)
                proj_group(bb, wk, wk_t, 128, xsrc, xtok)
                sl = slice(tc * 512, (tc + 1) * 512)
                rope(tc, (psb[ba][:, :], f"ps{ba}"), (psb[bb][:, :], f"ps{bb}"), KT[0][:, sl], KT[1][:, sl],
                     f"KA{tc}", f"KB{tc}")
                if tc >= 4:
                    ba, bb = pbank, pbank + 1
                    pbank = (pbank + 2) % 4
                    proj_group(ba, wq, wq_t, 0, xsrc, xtok)
                    proj_group(bb, wq, wq_t, 128, xsrc, xtok)
                    rope(tc, (psb[ba][:, :], f"ps{ba}"), (psb[bb][:, :], f"ps{bb}"), None, None, None, None,
                         qchunk=tc - 4)
                for tt in range(4):
                    kt = tc * 4 + tt
                    vb = 4 + (kt % 2)

                    def vfn(h, tt=tt, vb=vb, xsrc=xsrc, wv_=wv_):
                        ins = None
                        for dc in range(16):
                            ins = h.matmul(psb[vb][:, 0:256], lhsT=xsrc[:, dc, tt * 128:(tt + 1) * 128],
                                           rhs=wv_[:, dc, 0:256], start=(dc == 0), stop=(dc == 15))
                        return ins
                    S.op("pe", vfn, reads=[wv_t] + xtok, writes=[f"ps{vb}"])
                    S.op("act", lambda h, kt=kt, vb=vb: h.activation(
                        out=Vaug[:, kt, :, 0:128], in_=psb[vb][:, 0:256].rearrange("p (a b) -> p a b", b=128),
                        func=AF.Copy), reads=[f"ps{vb}"], writes=[f"V{kt}"])
                if tc < 4:
                    emit_xh(4 * g + tc + 2)
                    if g == 0 and tc == 1:
                        deferred_loads()

            W.release(3)
            units = [(qb, grp) for qb in range(8) for grp in range(9)]
            sbank_i = [0]
            ep_i = [0]
            state = {}

            def emit_qk(idx):
                qb, grp = units[idx]
                b = 16 + qb
                d0 = 2 * grp
                n = 2 if grp < 8 else 1
                bank = sbank_i[0] % 4
                sbank_i[0] += 1
                kts = [b - (d0 + i) for i in range(n)]
                rd = []
                for kt in kts:
                    rd += [f"KA{kt // 4}", f"KB{kt // 4}"]
                rd += [f"Q{qb // 4}"]
                def fn(h, kts=kts, bank=bank, qb=qb):
                    ins = None
                    for i, kt in enumerate(kts):
                        for s_ in range(2):
                            ins = h.matmul(psb[bank][:, i * 256:(i + 1) * 256],
                                           lhsT=KT[s_][:, kt * 128:(kt + 1) * 128],
                                           rhs=Qz[s_][:, qb, :, :].rearrange("p a b -> p (a b)"),
                                           start=(s_ == 0), stop=(s_ == 1))
                    return ins
                S.op("pe", fn, reads=list(dict.fromkeys(rd)), writes=[f"ps{bank}"])
                e = ep_i[0] % 4
                ep_i[0] += 1
                w = 2 * n * 128
                S.op("act", lambda h, bank=bank, w=w, e=e, n=n: h.activation(
                    out=Ebuf[e][:, 0:w].rearrange("p (a i q) -> p i a q", a=2, i=n),
                    in_=psb[bank][:, 0:w].rearrange("p (i a q) -> p i a q", a=2, i=n), func=AF.Exp, scale=SCL),
                    reads=[f"ps{bank}"], writes=[f"E{e}"])
                for hh_ in range(2):
                    S.op("dve", lambda h, n=n, e=e, d0=d0, hh_=hh_: h.tensor_tensor(
                        out=PTb[e][:, hh_ * n * 128:(hh_ + 1) * n * 128],
                        in0=Ebuf[e][:, hh_ * n * 128:(hh_ + 1) * n * 128],
                        in1=masks[:, d0 * 128:(d0 + n) * 128], op=ALU.mult),
                        reads=[f"E{e}", "masks"], writes=[f"PT{e}_{hh_}"])
                state[idx] = (e, kts, n)

            def emit_pv(idx):
                qb, grp = units[idx]
                e, kts, n = state.pop(idx)
                first = grp == 0
                last = grp == 8
                for hh_ in range(2):
                    ob = 4 + 2 * (qb % 2) + hh_

                    def fn(h, kts=kts, e=e, ob=ob, hh_=hh_, n=n):
                        ins = None
                        for i, kt in enumerate(kts):
                            ins = h.matmul(psb[ob][:, 0:129], lhsT=PTb[e][:, (hh_ * n + i) * 128:(hh_ * n + i + 1) * 128],
                                           rhs=Vaug[:, kt, hh_, :], start=(first and i == 0),
                                           stop=(last and i == len(kts) - 1))
                        return ins
                    S.op("pe", fn, reads=[f"PT{e}_{hh_}", f"Vval{hh_}"] + [f"V{kt}" for kt in kts],
                         writes=[f"ps{ob}"])
                    if last:
                        o = 2 * (qb % 2) + hh_
                        head = 2 * g + hh_
                        S.op("dve", lambda h, ob=ob, o=o: h.reciprocal(out=rLb[o][:, 0:1], in_=psb[ob][:, 128:129]),
                             reads=[f"ps{ob}"], writes=[f"rL{o}"])
                        S.op("dve", lambda h, ob=ob, o=o: h.tensor_scalar(
                            out=Onb[o], in0=psb[ob][:, 0:128], scalar1=rLb[o][:, 0:1], scalar2=None, op0=ALU.mult),
                            reads=[f"ps{ob}", f"rL{o}"], writes=[f"On{o}"])
                        tview = psb[ob][:, :].bitcast(BF16)[:, 512:640]

                        def late_fn(o=o, tview=tview, head=head, qb=qb, ob=ob):
                            S.op("pe", lambda h: h.transpose(out=tview, in_=Onb[o], identity=ident),
                                 reads=[f"On{o}", "ident"], writes=[f"ps{ob}"])
                            S.op("act", lambda h: h.activation(
                                out=OT[:, head, qb * 128:(qb + 1) * 128], in_=tview,
                                func=AF.Copy), reads=[f"ps{ob}"], writes=[f"OT{head}_{qb}"])
                        late.append((idx + 4, late_fn))

            ADEPTH = 3
            late = []

            def flush_late(now):
                while late and late[0][0] <= now:
                    late.pop(0)[1]()
            for idx in range(len(units)):
                emit_qk(idx)
                if idx >= ADEPTH:
                    emit_pv(idx - ADEPTH)
                    flush_late(idx - ADEPTH)
            for idx in range(max(0, len(units) - ADEPTH), len(units)):
                emit_pv(idx)
            flush_late(10 ** 9)

        if debug:
            S.dma("sp", "dbg", lambda h: [h.dma_start(out=dbg["OT"][:, :], in_=OT.rearrange("p a b -> p (a b)"))],
                  reads=[f"OT{hd}_{qb}" for hd in range(8) for qb in range(8)], writes=["dbgOT"])
        S.barrier()

        Cb = Bump(92672)
        mergedT = Cb.take((16, TOK), BF16)
        xh16 = Cb.take((16, 16), BF16)
        ubuf = [Cb.take((1040,), F32) for _ in range(3)]
        poolT = Cb.take((8, TOK), BF16)
        ypT = Cb.take((8, TOK), BF16)
        dtmp = [[Cb.take((512,), F32) for _ in range(4)] for _ in range(2)]
        t16 = Cb.take((16,), F32)
        assert Cb.o <= ARENA_ELEMS * 2

        S.dma("pool", "xh16", lambda h: [h.dma_start(out=xh16, in_=xT_v[:, :, HALO - 16:HALO])], writes=["xh16"])
        xo_tok = ["xTo0", "xTo1"]
        for j in range(8):
            if j % 2 == 0:
                wu, wu_t = W.use()
            c0 = (j % 2) * 128
            g = j // 2
            wwin = 2 ** (g + 1)
            for hf in range(2):
                proj_group(hf, wu, wu_t, c0, xTo[:, :, hf * 512:(hf + 1) * 512], [])

            def hfn(h, wu=wu, c0=c0):
                ins = None
                for dc in range(16):
                    ins = h.matmul(psb[2][:, 0:16], lhsT=wu[:, dc, c0:c0 + 128], rhs=xh16[:, dc, :],
                                   start=(dc == 0), stop=(dc == 15))
                return ins
            S.op("pe", hfn, reads=[wu_t, "xh16"], writes=["ps2"])
            u = ubuf[0]
            S.op("act", lambda h, u=u: h.activation(out=u[:, 16:528], in_=psb[0][:, :], func=AF.Copy),
                 reads=["ps0"], writes=["u_a"])
            S.op("act", lambda h, u=u: h.activation(out=u[:, 528:1040], in_=psb[1][:, :], func=AF.Copy),
                 reads=["ps1"], writes=["u_b"])
            S.op("act", lambda h, u=u: h.activation(out=u[:, 0:16], in_=psb[2][:, 0:16], func=AF.Copy),
                 reads=["ps2"], writes=["u_c"])
            src, srct = u, ["u_a", "u_b", "u_c"]
            for k in range(1, g + 2):
                dst = ubuf[1 + (k % 2)]
                dtok = f"ub{1 + (k % 2)}"
                t0 = 2 ** k - 1
                sh = 2 ** (k - 1)
                S.op("dve", lambda h, dst=dst, src=src, t0=t0, sh=sh: h.tensor_tensor(
                    out=dst[:, t0:1040], in0=src[:, t0:1040], in1=src[:, t0 - sh:1040 - sh], op=ALU.add),
                    reads=srct, writes=[dtok])
                src, srct = dst, [dtok]
            S.op("dve", lambda h, src=src, u=u, j=j, wwin=wwin: h.scalar_tensor_tensor(
                out=poolT[:, j, 16:TOK], in0=src[:, 32:1040], scalar=float(1.0 / wwin), in1=u[:, 32:1040],
                op0=ALU.mult, op1=ALU.subtract), reads=srct + ["u_a", "u_b"], writes=[f"pl{j}"])
            S.op("dve", lambda h, src=src, g=g: h.tensor_tensor(
                out=t16, in0=src[:, 16:32], in1=cvec[:, CV_INVC + g * 16:CV_INVC + (g + 1) * 16], op=ALU.mult),
                reads=srct, writes=["t16"])
            S.op("dve", lambda h, u=u, j=j: h.tensor_tensor(
                out=poolT[:, j, 0:16], in0=t16, in1=u[:, 16:32], op=ALU.subtract),
                reads=["t16", "u_a"], writes=[f"plh{j}"])
            if j % 2 == 1:
                W.release(1)
        for g in range(4):
            for oc in range(2):
                for hf in range(2):
                    bank = 4 + (g * 4 + oc * 2 + hf) % 2

                    def pfn(h, g=g, oc=oc, hf=hf, bank=bank):
                        ins = None
                        for kc in range(2):
                            ins = h.matmul(psb[bank][:, :], lhsT=wpool_sb[:, g, kc, oc * 128:(oc + 1) * 128],
                                           rhs=poolT[:, 2 * g + kc, hf * 512:(hf + 1) * 512],
                                           start=(kc == 0), stop=(kc == 1))
                        return ins
                    S.op("pe", pfn, reads=[f"pl{2 * g}", f"pl{2 * g + 1}", f"plh{2 * g}", f"plh{2 * g + 1}"],
                         writes=[f"ps{bank}"])
                    jo = 2 * g + oc
                    S.op("act", lambda h, jo=jo, hf=hf, bank=bank: h.activation(
                        out=ypT[:, jo, hf * 512:(hf + 1) * 512], in_=psb[bank][:, :], func=AF.Copy,
                        scale=cvec[:, CV_PSC + jo:CV_PSC + jo + 1]), reads=[f"ps{bank}"], writes=[f"yp{jo}_{hf}"])
        if debug:
            S.dma("sp", "dbg", lambda h: [h.dma_start(out=dbg["yp"][:, :], in_=ypT.rearrange("p a b -> p (a b)"))],
                  reads=[f"yp{jo}_{hf}" for jo in range(8) for hf in range(2)], writes=["dbgyp"])

        for j in range(16):
            wg, wg_t = W.use()
            wb, wb_t = W.use()
            for hf in range(2):
                bs = 4 * hf
                tsl = slice(hf * 512, (hf + 1) * 512)
                proj_group(bs + 0, wg, wg_t, 0, xTo[:, :, tsl], [])
                proj_group(bs + 1, wg, wg_t, 128, xTo[:, :, tsl], [])

                def yfn(h, bank, base, src, wb=wb, tsl=tsl):
                    ins = None
                    for c in range(8):
                        ins = h.matmul(psb[bank][:, :], lhsT=wb[:, base + c, 0:128], rhs=src[:, c, tsl],
                                       start=(c == 0), stop=(c == 7))
                    return ins
                S.op("pe", lambda h, bs=bs, yfn=yfn: yfn(h, bs + 2, 0, OT), reads=[wb_t, "OTreg"],
                     writes=[f"ps{bs + 2}"])
                S.op("pe", lambda h, bs=bs, yfn=yfn: yfn(h, bs + 3, 8, ypT),
                     reads=[wb_t] + [f"yp{jo}_{hf}" for jo in range(8)], writes=[f"ps{bs + 3}"])
                sA, sB, m1, m2 = dtmp[hf]
                S.op("act", lambda h, bs=bs, sA=sA: h.activation(out=sA, in_=psb[bs][:, :], func=AF.Sigmoid),
                     reads=[f"ps{bs}"], writes=[f"sA{hf}"])
                S.op("act", lambda h, bs=bs, sB=sB: h.activation(out=sB, in_=psb[bs + 1][:, :], func=AF.Sigmoid),
                     reads=[f"ps{bs + 1}"], writes=[f"sB{hf}"])
                S.op("dve", lambda h, bs=bs, sA=sA, m1=m1: h.tensor_tensor(out=m1, in0=sA, in1=psb[bs + 2][:, :],
                                                                          op=ALU.mult),
                     reads=[f"sA{hf}", f"ps{bs + 2}"], writes=[f"m1{hf}"])
                S.op("dve", lambda h, bs=bs, sB=sB, m2=m2: h.tensor_tensor(out=m2, in0=sB, in1=psb[bs + 3][:, :],
                                                                          op=ALU.mult),
                     reads=[f"sB{hf}", f"ps{bs + 3}"], writes=[f"m2{hf}"])
                S.op("dve", lambda h, m1=m1, m2=m2, j=j, tsl=tsl: h.tensor_tensor(
                    out=mergedT[:, j, tsl], in0=m1, in1=m2, op=ALU.add),
                    reads=[f"m1{hf}", f"m2{hf}"], writes=[f"mg{j}_{hf}"])
            W.release(2)
        if debug:
            S.dma("sp", "dbg", lambda h: [h.dma_start(out=dbg["mg"][:, :], in_=mergedT.rearrange("p a b -> p (a b)"))],
                  reads=[f"mg{j}_{hf}" for j in range(16) for hf in range(2)], writes=["dbgmg"])

        x1f = carve(125440, (16, TOK), F32)
        x1b = carve(43520, (16, TOK), BF16)
        Eb = Bump(76288)
        xf = [Eb.take((512,), F32) for _ in range(4)]
        sqb = [Eb.take((512,), F32) for _ in range(2)]
        SS = [[None, None], [None, None]]
        SS[0][0] = Eb.take((512,), F32)
        SS[0][1] = Eb.take((512,), F32)
        assert Eb.o <= 92672
        Ub = Bump(190976)
        lnA = Ub.take((TOK,), F32)
        lnB = Ub.take((TOK,), F32)
        lt = [Ub.take((512,), F32) for _ in range(2)]
        SS[1][0] = Ub.take((512,), F32)
        SS[1][1] = Ub.take((512,), F32)
        assert Ub.o <= ARENA_ELEMS * 2

        def stats_accum(hf, j, zap, ztok):
            s1, s2 = SS[hf]
            t1, t2 = f"S1_{hf}", f"S2_{hf}"
            if j == 0:
                S.op("act", lambda h: h.activation(out=s1, in_=zap, func=AF.Copy), reads=[ztok], writes=[t1])
                S.op("act", lambda h: h.activation(out=s2, in_=zap, func=AF.Square), reads=[ztok], writes=[t2])
            else:
                q = j % 2
                S.op("dve", lambda h: h.tensor_tensor(out=s1, in0=s1, in1=zap, op=ALU.add),
                     reads=[ztok, t1], writes=[t1])
                S.op("act", lambda h: h.activation(out=sqb[q], in_=zap, func=AF.Square),
                     reads=[ztok], writes=[f"sq{q}"])
                S.op("dve", lambda h: h.tensor_tensor(out=s2, in0=s2, in1=sqb[q], op=ALU.add),
                     reads=[f"sq{q}", t2], writes=[t2])

        def stats_final(hf, b1, b2):
            s1, s2 = SS[hf]
            S.op("pe", lambda h: h.matmul(psb[b1][:, :], lhsT=ones_f, rhs=s1, start=True, stop=True),
                 reads=[f"S1_{hf}", "ones_f"], writes=[f"ps{b1}"])
            S.op("pe", lambda h: h.matmul(psb[b2][:, :], lhsT=ones_f, rhs=s2, start=True, stop=True),
                 reads=[f"S2_{hf}", "ones_f"], writes=[f"ps{b2}"])

        def ln_tables(hf, bsum, bsq):
            tsl = slice(hf * 512, (hf + 1) * 512)
            mean, ex2 = lt
            S.op("act", lambda h: h.activation(out=mean, in_=psb[bsum][:, :], func=AF.Copy, scale=float(1.0 / D)),
                 reads=[f"ps{bsum}"], writes=["lt0"])
            S.op("act", lambda h: h.activation(out=ex2, in_=psb[bsq][:, :], func=AF.Copy, scale=float(1.0 / D)),
                 reads=[f"ps{bsq}"], writes=["lt1"])
            S.op("dve", lambda h: h.tensor_tensor(out=lnB[:, tsl], in0=mean, in1=mean, op=ALU.mult),
                 reads=["lt0"], writes=[f"lnB{hf}"])
            S.op("dve", lambda h: h.tensor_tensor(out=ex2, in0=ex2, in1=lnB[:, tsl], op=ALU.subtract),
                 reads=["lt1", f"lnB{hf}"], writes=["lt1"])
            S.op("dve", lambda h: h.tensor_scalar(out=ex2, in0=ex2, scalar1=float(LN_EPS), scalar2=None,
                                                  op0=ALU.add), reads=["lt1"], writes=["lt1"])
            S.op("act", lambda h: h.activation(out=ex2, in_=ex2, func=AF.Sqrt), reads=["lt1"], writes=["lt1"])
            S.op("dve", lambda h: h.reciprocal(out=lnA[:, tsl], in_=ex2), reads=["lt1"], writes=[f"lnA{hf}"])
            S.op("dve", lambda h: h.scalar_tensor_tensor(out=lnB[:, tsl], in0=mean, scalar=-1.0, in1=lnA[:, tsl],
                                                         op0=ALU.mult, op1=ALU.mult),
                 reads=["lt0", f"lnA{hf}"], writes=[f"lnB{hf}"])

        def ln_apply(j, hf, buf, ztok, gcol, bcol, out_tok, bf_out=None):
            tsl = slice(hf * 512, (hf + 1) * 512)
            zz = buf[:, j, tsl]
            g_ap = cvec[:, gcol + j:gcol + j + 1]
            b_ap = cvec[:, bcol + j:bcol + j + 1]
            S.op("dve", lambda h: h.tensor_tensor(out=zz, in0=zz, in1=lnA[:, tsl], op=ALU.mult),
                 reads=[ztok, f"lnA{hf}"], writes=[ztok])
            S.op("dve", lambda h: h.tensor_tensor(out=zz, in0=zz, in1=lnB[:, tsl], op=ALU.add),
                 reads=[ztok, f"lnB{hf}"], writes=[ztok])
            if bf_out is not None:
                S.op("act", lambda h: h.activation(out=bf_out[:, j, tsl], in_=zz, func=AF.Identity,
                                                   scale=g_ap, bias=b_ap),
                     reads=[ztok], writes=[out_tok])
            S.op("act", lambda h: h.activation(out=zz, in_=zz, func=AF.Identity, scale=g_ap, bias=b_ap),
                 reads=[ztok] + ([out_tok] if bf_out is not None else []), writes=[ztok])

        def stats_mm(bank, src, srctok, first, last):
            S.op("pe", lambda h: h.matmul(psb[bank][:, :], lhsT=ones_f, rhs=src, start=first, stop=last),
                 reads=[srctok], writes=[f"ps{bank}"])

        import collections
        pending = collections.deque()

        def drain(n):
            for _ in range(n):
                if pending:
                    pending.popleft()()

        e_iters = [(hf, jp, jj) for hf in range(2) for jp in range(8) for jj in range(2)]

        def ln1_sched(hf):
            stats_final(hf, 4 + hf, 6 + hf)
            pending.append(lambda: ln_tables(hf, 4 + hf, 6 + hf))
            for jx in range(16):
                pending.append(lambda jx=jx: ln_apply(jx, hf, x1f, f"z{jx}_{hf}", CV_LN1G, CV_LN1B,
                                                      f"x1b{jx}_{hf}", bf_out=x1b))

        def emit_xf(i):
            hf, jp, jj = e_iters[i]
            j = 2 * jp + jj
            s = i % 4
            S.dma("sp", f"xf{s}", lambda h, s=s, j=j, hf=hf: [h.dma_start(
                out=xf[s], in_=xT[j * 128:(j + 1) * 128, HALO + hf * 512: HALO + (hf + 1) * 512])],
                writes=[f"xf{s}"] + (["OTreg"] if i < 4 else []))
        emit_xf(0)
        emit_xf(1)
        wo = wo_t = None
        for i, (hf, jp, jj) in enumerate(e_iters):
            if jj == 0:
                wo, wo_t = W.use()
            if i + 2 < len(e_iters):
                emit_xf(i + 2)
            j = 2 * jp + jj
            tsl = slice(hf * 512, (hf + 1) * 512)
            bank = i % 4

            def mfn(h, wo=wo, jj=jj, tsl=tsl, bank=bank):
                ins = None
                for kc in range(16):
                    ins = h.matmul(psb[bank][:, :], lhsT=wo[:, kc, jj * 128:(jj + 1) * 128], rhs=mergedT[:, kc, tsl],
                                   start=(kc == 0), stop=(kc == 15))
                return ins
            S.op("pe", mfn, reads=[wo_t] + [f"mg{kc}_{hf}" for kc in range(16)], writes=[f"ps{bank}"])
            s_ = i % 4
            ztok = f"z{j}_{hf}"
            S.op("dve", lambda h, s_=s_, j=j, tsl=tsl, bank=bank: h.scalar_tensor_tensor(
                out=x1f[:, j, tsl], in0=xf[s_], scalar=ALPHA, in1=psb[bank][:, :], op0=ALU.mult, op1=ALU.add),
                reads=[f"xf{s_}", f"ps{bank}"], writes=[ztok])
            stats_accum(hf, j, x1f[:, j, tsl], ztok)
            if jj == 1:
                W.release(1)
            drain(1)
            if i == 17:
                ln1_sched(0)
                drain(1)
        drain(len(pending))
        if debug:
            S.dma("sp", "dbg", lambda h: [h.dma_start(out=dbg["x1"][:, :], in_=x1f.rearrange("p a b -> p (a b)"))],
                  reads=[f"z{j}_{hf}" for j in range(16) for hf in range(2)], writes=["dbgx1"])

        hTg = [carve(92672 + b_ * 16384, (8, TOK), BF16) for b_ in range(2)]
        Fb = Bump(76288)
        rb = [Fb.take((512,), F32) for _ in range(2)]
        assert Fb.o <= 84480
        fbank = [0]
        rcnt = [0]

        def nextbank():
            b_ = fbank[0] % 4
            fbank[0] += 1
            return b_

        def f1_one(hg, i, cc, th, w1s, w1_t, ndrain):
            buf = hTg[hg % 2]
            hc = 2 * i + cc
            bank = nextbank()
            tsl = slice(th * 512, (th + 1) * 512)

            def f1(h):
                ins = None
                for kc in range(16):
                    ins = h.matmul(psb[bank][:, :], lhsT=w1s[:, kc, cc * 128:(cc + 1) * 128],
                                   rhs=x1b[:, kc, tsl], start=(kc == 0), stop=(kc == 15))
                return ins
            S.op("pe", f1, reads=[w1_t] + [f"x1b{jx}_{th}" for jx in range(16)], writes=[f"ps{bank}"])
            r = rcnt[0] % 2
            rcnt[0] += 1
            S.op("act", lambda h: h.activation(out=rb[r], in_=psb[bank][:, :], func=AF.Relu),
                 reads=[f"ps{bank}"], writes=[f"r{r}"])
            S.op("dve", lambda h: h.tensor_tensor(out=buf[:, hc, tsl], in0=rb[r], in1=rb[r], op=ALU.mult),
                 reads=[f"r{r}"], writes=[f"h{hg % 2}_{hc}_{th}"])
            drain(ndrain)

        f1_count = [0]

        def emit_F1_half(hg, th):
            for i in range(4):
                w1s, w1_t = W.use()
                for cc in range(2):
                    f1_one(hg, i, cc, th, w1s, w1_t, 1)
                    f1_count[0] += 1
                    if f1_count[0] == 2:
                        ln1_sched(1)
                        drain(1)
                W.release(1)

        def emit_F1(hg):
            if hg == 0:
                raise AssertionError
            else:
                for i in range(4):
                    w1s, w1_t = W.use()
                    for cc in range(2):
                        for th in range(2):
                            f1_one(hg, i, cc, th, w1s, w1_t, 1)
                    W.release(1)

        def f2_group(hg, j, jj, th, w2s, w2_t):
            buf = hTg[hg % 2]
            bank = nextbank()
            tsl = slice(th * 512, (th + 1) * 512)

            def f2(h):
                ins = None
                for hc in range(8):
                    ins = h.matmul(psb[bank][:, :], lhsT=w2s[:, hc, jj * 128:(jj + 1) * 128],
                                   rhs=buf[:, hc, tsl], start=(hc == 0), stop=(hc == 7))
                return ins
            S.op("pe", f2, reads=[w2_t] + [f"h{hg % 2}_{hc}_{th}" for hc in range(8)], writes=[f"ps{bank}"])
            ztok = f"z{j}_{th}"
            if hg == 0:
                S.op("dve", lambda h: h.scalar_tensor_tensor(
                    out=x1f[:, j, tsl], in0=x1f[:, j, tsl], scalar=ALPHA, in1=psb[bank][:, :],
                    op0=ALU.mult, op1=ALU.add), reads=[f"ps{bank}", ztok], writes=[ztok])
            else:
                S.op("dve", lambda h: h.tensor_tensor(
                    out=x1f[:, j, tsl], in0=x1f[:, j, tsl], in1=psb[bank][:, :], op=ALU.add),
                    reads=[f"ps{bank}", ztok], writes=[ztok])
            if hg == 7:
                stats_accum(th, j, x1f[:, j, tsl], ztok)
                drain(1)

        def ln2_sched(th):
            tsl = slice(th * 512, (th + 1) * 512)
            stats_final(th, 4 + th, 6 + th)
            pending.append(lambda: ln_tables(th, 4 + th, 6 + th))

            def ln2_apply(j):
                ln_apply(j, th, x1f, f"z{j}_{th}", CV_LN2G, CV_LN2B, None)
                if j % 4 == 3:
                    S.dma("sp", "out", lambda h: [h.dma_start(out=outT_v[:, j - 3:j + 1, tsl],
                                                              in_=x1f[:, j - 3:j + 1, tsl])],
                          reads=[f"z{jx}_{th}" for jx in range(j - 3, j + 1)], writes=[f"out{th}_{j // 4}"])
            for j in range(16):
                pending.append(lambda j=j: ln2_apply(j))
            drain(1)

        def emit_F2(hg):
            if hg < 7:
                for jp in range(8):
                    w2s, w2_t = W.use()
                    for jj in range(2):
                        for th in range(2):
                            f2_group(hg, 2 * jp + jj, jj, th, w2s, w2_t)
                    W.release(1)
            else:
                for th in range(2):
                    for jp in range(8):
                        w2s, w2_t = W.use()
                        for jj in range(2):
                            f2_group(hg, 2 * jp + jj, jj, th, w2s, w2_t)
                        W.release(1)
                        if th == 1 and jp == 0:
                            ln2_sched(0)
                ln2_sched(1)

        emit_F1_half(0, 0)
        emit_F1_half(1, 0)
        drain(len(pending))
        emit_F1_half(0, 1)
        emit_F1_half(1, 1)
        emit_F2(0)
        for hg in range(2, 8):
            emit_F1(hg)
            emit_F2(hg - 1)
        emit_F2(7)
        drain(len(pending))
        S.barrier()
        S.emit(block)
        build_nc.last_sched = S
    return nc


def _host_prep(x, positions, w_in, w_pool, pool_scale, w_branch_attn, w_branch_pool, w_out,
               ln_mix_g, ln_mix_b, w_ff1, w_ff2, ln_ff_g, ln_ff_b):
    f32 = np.float32
    x2 = np.asarray(x, f32)[0]
    xTfull = np.ascontiguousarray(x2.T)
    posf = np.asarray(positions, np.int32)[0]
    W = np.asarray(w_in, f32)[0]
    cols = []
    for g in range(4):
        a, b = 2 * g, 2 * g + 1
        for base in (0, 1024):
            A = np.concatenate([np.arange(base + a * 128, base + a * 128 + 64),
                                np.arange(base + b * 128, base + b * 128 + 64)])
            Bc = np.concatenate([np.arange(base + a * 128 + 64, base + a * 128 + 128),
                                 np.arange(base + b * 128 + 64, base + b * 128 + 128)])
            cols += [A, Bc]
        cols.append(np.arange(2048 + a * 128, 2048 + a * 128 + 256))
    w_att = np.ascontiguousarray(W[:, np.concatenate(cols)])
    w_u = np.ascontiguousarray(W[:, 3072:4096])
    gcols = []
    for j in range(16):
        gcols.append(np.arange(4096 + j * 128, 4096 + (j + 1) * 128))
        gcols.append(np.arange(6144 + j * 128, 6144 + (j + 1) * 128))
    w_g = np.ascontiguousarray(W[:, np.concatenate(gcols)])
    w_b = np.ascontiguousarray(np.concatenate([np.asarray(w_branch_attn, f32)[0],
                                               np.asarray(w_branch_pool, f32)[0]], axis=0))
    shared = dict(
        w_att=w_att, w_u=w_u, w_g=w_g, w_b=w_b,
        w_out=np.ascontiguousarray(np.asarray(w_out, f32)[0]),
        w1=np.ascontiguousarray(np.asarray(w_ff1, f32)[0]),
        w2=np.ascontiguousarray(np.asarray(w_ff2, f32)[0]),
        w_pool=np.ascontiguousarray(np.asarray(w_pool, f32)[0]),
    )
    k = np.arange(128)[:, None]
    q = np.arange(128)[None, :]
    masks = np.zeros((128, NDELTA, 128), f32)
    for dl in range(NDELTA):
        dist = 128 * dl + q - k
        m = ((dist >= 0) & (dist <= 128)).astype(f32)
        m += ((dist >= 0) & (dist <= 512) & (dist % 4 == 0)).astype(f32)
        m += ((dist >= 0) & (dist <= 2048) & (dist % 16 == 0)).astype(f32)
        masks[:, dl, :] = m
    shared["masks"] = np.ascontiguousarray(masks.reshape(128, NDELTA * 128))
    shared["ident"] = np.eye(128, dtype=f32)
    half = 64
    inv_freq = np.power(f32(10000.0), -(np.arange(half, dtype=f32) / f32(half))).astype(f32)

    def pj(v, n):
        return np.asarray(v, f32).reshape(n, 128).T

    in_maps = []
    for c in range(NCORES):
        lo = c * TOK - HALO
        xw = np.zeros((D, WIN), f32)
        pw = np.zeros((1, WIN), np.int32)
        s0 = max(lo, 0)
        xw[:, s0 - lo:] = xTfull[:, s0:(c + 1) * TOK]
        pw[0, s0 - lo:] = posf[s0:(c + 1) * TOK]
        cv = np.zeros((128, CV_N), f32)
        cv[:, CV_INVF] = np.concatenate([inv_freq, inv_freq])
        gp = lo + np.arange(NKT)[None, :] * 128 + np.arange(128)[:, None]
        cv[:, CV_VALID:CV_VALID + NKT] = (gp >= 0).astype(f32)
        for g, wwin in enumerate((2, 4, 8, 16)):
            cnt = np.minimum(c * TOK + np.arange(16) + 1, wwin).astype(f32)
            cv[:, CV_INVC + g * 16:CV_INVC + (g + 1) * 16] = (f32(1.0) / cnt)[None, :]
        cv[:, CV_PSC:CV_PSC + 8] = pj(np.asarray(pool_scale)[0], 8)
        cv[:, CV_LN1G:CV_LN1G + 16] = pj(np.asarray(ln_mix_g)[0], 16)
        cv[:, CV_LN1B:CV_LN1B + 16] = pj(np.asarray(ln_mix_b)[0], 16)
        cv[:, CV_LN2G:CV_LN2G + 16] = pj(np.asarray(ln_ff_g)[0], 16)
        cv[:, CV_LN2B:CV_LN2B + 16] = pj(np.asarray(ln_ff_b)[0], 16)
        m = dict(shared)
        m.update(xT=xw, pos=pw, cvec=cv)
        in_maps.append(m)
    return in_maps


def kernel(**inputs):
    in_maps = _host_prep(**inputs)
    nc = build_nc()
    res = run_bass_kernel_spmd(nc, in_maps, core_ids=list(range(NCORES)))
    out = np.empty((1, S_TOT, D), np.float32)
    for c in range(NCORES):
        out[0, c * TOK:(c + 1) * TOK, :] = np.asarray(res.results[c]["outT"], np.float32).T
    return out
```

```python
import contextlib
import numpy as np
import concourse.bass as bass
import concourse.mybir as mybir
from concourse.bass_utils import run_bass_kernel_spmd

F32 = mybir.dt.float32
BF16 = mybir.dt.bfloat16
I32 = mybir.dt.int32
AF = mybir.ActivationFunctionType
ALU = mybir.AluOpType

NCORES = 8
D = 2048
S_TOT = 8192
TOK = 1024
HALO = 2048
WIN = TOK + HALO
NKT = WIN // 128
DFF = 8192
ALPHA = float(2.0 ** 0.25)
LN_EPS = 1e-5
NDELTA = 17
TWO_PI_HI = 6.28125
TWO_PI_LO = 0.0019353071795864769
ARENA_ELEMS = 105472

CV_INVF = 0
CV_VALID = 1
CV_INVC = CV_VALID + NKT
CV_PSC = CV_INVC + 64
CV_LN1G = CV_PSC + 8
CV_LN1B = CV_LN1G + 16
CV_LN2G = CV_LN1B + 16
CV_LN2B = CV_LN2G + 16
CV_N = CV_LN2B + 16


class Sched:
    def __init__(self, nc, sems):
        self.nc = nc
        self.eng = {}
        for name in ("pe", "act", "dve", "pool", "sp"):
            self.eng[name] = dict(cnt=0, ops=[], known={}, log=[])
        self.lastw = {}
        self.reads = {}
        self.semobj = dict(sems)
        self.dmacnt = {}

    def _waits(self, e, reads, writes):
        evs = []
        for t in reads:
            if t in self.lastw:
                evs.append(self.lastw[t])
        for t in writes:
            if t in self.lastw:
                evs.append(self.lastw[t])
            evs.extend(self.reads.get(t, []))
        E = self.eng[e]
        need = {}
        for (sk, v) in evs:
            if E["known"].get(sk, 0) >= v:
                continue
            need[sk] = max(need.get(sk, 0), v)
        for sk, v in need.items():
            E["known"][sk] = v
        return list(need.items())

    def _record(self, ev, reads, writes):
        for t in reads:
            self.reads.setdefault(t, []).append(ev)
        for t in writes:
            self.lastw[t] = ev
            self.reads[t] = []

    def op(self, e, fn, reads=(), writes=()):
        E = self.eng[e]
        waits = self._waits(e, reads, writes)
        if e == "pe":
            waits = [(sk, v) for (sk, v) in waits if sk != "pe"]
        E["cnt"] += 1
        ev = (e, E["cnt"])
        semobj = self.semobj

        def run(h, waits=waits, fn=fn, e=e):
            for sk, v in waits:
                h.wait_ge(semobj[sk], v)
            fn(h).then_inc(semobj[e], 1)
        E["ops"].append(run)
        E["log"].append((list(waits), (e, 1)))
        self._record(ev, reads, writes)
        return ev

    def dma(self, e, semkey, fn, reads=(), writes=(), n=1):
        E = self.eng[e]
        waits = self._waits(e, reads, writes)
        self.dmacnt[semkey] = self.dmacnt.get(semkey, 0) + 16 * n
        ev = (semkey, self.dmacnt[semkey])
        semobj = self.semobj

        def run(h, waits=waits, fn=fn, semkey=semkey):
            for sk, v in waits:
                h.wait_ge(semobj[sk], v)
            for ins in fn(h):
                ins.then_inc(semobj[semkey], 16)
        E["ops"].append(run)
        E["log"].append((list(waits), (semkey, 16 * n)))
        self._record(ev, reads, writes)
        return ev

    def barrier(self):
        evs = [(e, E["cnt"]) for e, E in self.eng.items() if E["cnt"] > 0]
        evs += list(self.dmacnt.items())
        semobj = self.semobj
        for e, E in self.eng.items():
            waits = [(sk, v) for (sk, v) in evs if E["known"].get(sk, 0) < v]
            for sk, v in waits:
                E["known"][sk] = v

            def run(h, waits=waits):
                for sk, v in waits:
                    h.wait_ge(semobj[sk], v)
            E["ops"].append(run)
            E["log"].append((list(waits), None))
        self.lastw.clear()
        self.reads.clear()

    def emit(self, block):
        def mk(name):
            def body(h):
                for r in self.eng[name]["ops"]:
                    r(h)
            return body
        block.tensor(mk("pe"))
        block.scalar(mk("act"))
        block.vector(mk("dve"))
        block.gpsimd(mk("pool"))
        block.sync(mk("sp"))


def build_nc(debug=False):
    nc = bass.Bass("TRN2", target_bir_lowering=False)

    def din(name, shape, dt=F32):
        return nc.dram_tensor(name, list(shape), dt, kind="ExternalInput").ap()

    xT = din("xT", [D, WIN])
    pos = din("pos", [1, WIN], I32)
    w_att = din("w_att", [D, 3072])
    w_u = din("w_u", [D, 1024])
    w_g = din("w_g", [D, 4096])
    w_b = din("w_b", [D, 2048])
    w_out = din("w_out", [D, D])
    w1 = din("w1", [D, DFF])
    w2 = din("w2", [DFF, D])
    w_pool = din("w_pool", [4, 256, 256])
    masks_d = din("masks", [128, NDELTA * 128])
    ident_d = din("ident", [128, 128])
    cvec_d = din("cvec", [128, CV_N])
    outT = nc.dram_tensor("outT", [D, TOK], F32, kind="ExternalOutput").ap()
    dbg = {}
    if debug:
        dbg["OT"] = nc.dram_tensor("dbg_OT", [128, 8 * TOK], BF16, kind="ExternalOutput").ap()
        dbg["yp"] = nc.dram_tensor("dbg_yp", [128, 8 * TOK], BF16, kind="ExternalOutput").ap()
        dbg["mg"] = nc.dram_tensor("dbg_mg", [128, 16 * TOK], BF16, kind="ExternalOutput").ap()
        dbg["x1"] = nc.dram_tensor("dbg_x1", [128, 16 * TOK], F32, kind="ExternalOutput").ap()

    def kview(w):
        return w.rearrange("(k p) c -> p k c", p=128)

    xT_v = kview(xT)
    watt_v, wu_v, wg_v, wb_v, wout_v, w1_v, w2_v = (kview(w) for w in (w_att, w_u, w_g, w_b, w_out, w1, w2))
    outT_v = kview(outT)

    with contextlib.ExitStack() as st:
        arena = st.enter_context(nc.sbuf_tensor("arena", [128, ARENA_ELEMS], BF16))
        psb = [st.enter_context(nc.psum_tensor(f"ps{i}", [128, 512], F32)) for i in range(8)]
        semnames = ["pe", "act", "dve", "pool", "sp", "w0", "w1", "w2", "w3", "xh0", "xh1", "xo0", "xo1", "cst0", "cst1", "cst2", "cvs", "xh16", "misc",
                    "xf0", "xf1", "xf2", "xf3", "out", "dbg"]
        sems = {k: st.enter_context(nc.semaphore(k)) for k in semnames}
        block = st.enter_context(nc.Block())
        S = Sched(nc, sems)

        def carve(off, dims, dt):
            n = int(np.prod(dims))
            esz = 2 if dt == BF16 else 4
            assert off % 4 == 0 and off + n * esz <= ARENA_ELEMS * 2, (off, dims)
            ap = arena[:, off // 2: off // 2 + n * esz // 2]
            if dt != BF16:
                ap = ap.bitcast(dt)
            if len(dims) == 2:
                ap = ap.rearrange("p (a b) -> p a b", b=dims[1])
            elif len(dims) == 3:
                ap = ap.rearrange("p (a b c) -> p a b c", b=dims[1], c=dims[2])
            return ap

        class Bump:
            def __init__(self, base):
                self.o = base

            def take(self, dims, dt):
                n = int(np.prod(dims)) * (2 if dt == BF16 else 4)
                n4 = (n + 31) // 32 * 32
                ap = carve(self.o, dims, dt)
                self.o += n4
                return ap

        P = Bump(0)
        ring = [P.take((4096,), BF16) for _ in range(4)]
        masks = P.take((NDELTA * 128,), BF16)
        ident = P.take((128,), BF16)
        ones_bf = P.take((128,), BF16)
        ones_f = P.take((128,), F32)
        cvec = P.take((CV_N,), F32)
        wpool_sb = P.take((4, 2, 256), BF16)
        assert P.o <= 43520, P.o
        L1 = Bump(43520)
        xTo = L1.take((16, TOK), BF16)
        OT = L1.take((8, TOK), BF16)
        assert L1.o == 92672

        loads = []
        for g in range(4):
            for i in (1, 2, 0):
                loads.append(((16, 256), watt_v[:, :, g * 768 + i * 256: g * 768 + (i + 1) * 256]))
        for i in range(4):
            loads.append(((16, 256), wu_v[:, :, i * 256:(i + 1) * 256]))
        for j in range(16):
            loads.append(((16, 256), wg_v[:, :, j * 256:(j + 1) * 256]))
            loads.append(((16, 128), wb_v[:, :, j * 128:(j + 1) * 128]))
        for hf_ in range(2):
            for jp in range(8):
                loads.append(((16, 256), wout_v[:, :, jp * 256:(jp + 1) * 256]))
        def _w1_loads(hg):
            for i in range(4):
                c0 = hg * 1024 + i * 256
                loads.append(((16, 256), w1_v[:, :, c0:c0 + 256]))

        def _w2_loads(hg):
            for jp in range(8):
                loads.append(((8, 256), w2_v[:, hg * 8:(hg + 1) * 8, jp * 256:(jp + 1) * 256]))
        _w1_loads(0)
        _w1_loads(1)
        _w1_loads(0)
        _w1_loads(1)
        _w2_loads(0)
        for hg in range(2, 8):
            _w1_loads(hg)
            _w2_loads(hg - 1)
        _w2_loads(7)
        _w2_loads(7)

        class WStream:
            def __init__(self):
                self.emitted = 0
                self.released = 0
                self.next_use = 0

            def _emit_load(self, i, after=()):
                dims, src = loads[i]
                s = i % 4
                dst = ring[s][:, 0:dims[0] * dims[1]].rearrange("p (a b) -> p a b", b=dims[1])
                S.dma("pool", f"w{s}", lambda h, dst=dst, src=src: [h.dma_start(out=dst, in_=src)],
                      reads=list(after), writes=[f"w{s}"])

            def topup(self, cap=4):
                while self.emitted < min(self.released + cap, len(loads)):
                    self._emit_load(self.emitted)
                    self.emitted += 1

            def use(self):
                self.topup()
                i = self.next_use
                self.next_use += 1
                assert i < self.emitted, (i, self.emitted, self.released)
                dims, _ = loads[i]
                s = i % 4
                view = ring[s][:, 0:dims[0] * dims[1]].rearrange("p (a b) -> p a b", b=dims[1])
                return view, f"w{s}"

            def release(self, n=1):
                self.released += n
                self.topup()

        W = WStream()

        S.dma("sp", "cvs", lambda h: [h.dma_start(out=cvec, in_=cvec_d[:, :])], writes=["cvec"])
        def deferred_loads():
            for hf in range(2):
                S.dma("pool", f"xo{hf}", lambda h, hf=hf: [h.dma_start(
                    out=xTo[:, :, hf * 512:(hf + 1) * 512], in_=xT_v[:, :, HALO + hf * 512: HALO + (hf + 1) * 512])],
                    writes=[f"xTo{hf}"])
            S.dma("pool", "cst0", lambda h: [h.dma_start(out=masks, in_=masks_d[:, :])], writes=["masks"])
            S.dma("pool", "cst1", lambda h: [h.dma_start(out=ident, in_=ident_d[:, :])], writes=["ident"])
            S.dma("pool", "cst2", lambda h: [h.dma_start(
                out=wpool_sb, in_=w_pool.rearrange("g (kc p) o -> p g kc o", p=128))], writes=["wpool"])
        S.op("dve", lambda h: h.memset(ones_bf, 1.0), writes=["ones_bf"])
        S.op("dve", lambda h: h.memset(ones_f, 1.0), writes=["ones_f"])

        B = Bump(92672)
        xh = [B.take((16, 512), BF16) for _ in range(2)]
        cosT = B.take((WIN,), F32)
        sinT = B.take((WIN,), F32)
        KT = [B.take((WIN,), BF16) for _ in range(2)]
        Qz = [B.take((8, 2, 128), BF16) for _ in range(2)]
        Vaug = B.take((NKT, 2, 129), BF16)
        rt = [B.take((512,), F32) for _ in range(4)]
        EP = B.take((4096,), BF16)
        Ebuf = [EP[:, k_ * 512:(k_ + 1) * 512] for k_ in range(4)]
        PTb = [EP[:, 2048 + k_ * 512:2048 + (k_ + 1) * 512] for k_ in range(4)]
        Onb = [B.take((128,), BF16) for _ in range(4)]
        rLb = [B.take((8,), F32) for _ in range(4)]
        posi = B.take((512,), I32)
        st_f = [EP[:, k_ * 1024:(k_ + 1) * 1024].bitcast(F32) for k_ in range(4)]
        st_i = B.take((512,), I32)
        assert B.o <= ARENA_ELEMS * 2, B.o

        for s_ in range(2):
            S.op("dve", lambda h, s_=s_: h.memset(Qz[s_].rearrange("p a b c -> p (a b c)"), 0.0), writes=[f"Qz{s_}"])
        for hh_ in range(2):
            S.op("dve", lambda h, hh_=hh_: h.tensor_copy(out=Vaug[:, :, hh_, 128], in_=cvec[:, CV_VALID:CV_VALID + NKT]),
                 reads=["cvec"], writes=[f"Vval{hh_}"])

        invf = cvec[:, CV_INVF:CV_INVF + 1]
        for tc in range(6):
            sl = slice(tc * 512, (tc + 1) * 512)
            S.dma("sp", "misc", lambda h, tc=tc: [h.dma_start(
                out=posi, in_=pos[0:1, tc * 512:(tc + 1) * 512].partition_broadcast(128))], writes=["posi"])
            pf, ang, tq, kf = st_f
            S.op("dve", lambda h: h.tensor_copy(out=pf, in_=posi), reads=["posi"], writes=["pf"])
            S.op("dve", lambda h: h.tensor_scalar(out=ang, in0=pf, scalar1=invf, scalar2=None, op0=ALU.mult),
                 reads=["pf", "cvec"], writes=["ang"])
            S.op("dve", lambda h: h.tensor_scalar(out=tq, in0=ang, scalar1=float(1.0 / (2.0 * np.pi)), scalar2=None,
                                                  op0=ALU.mult), reads=["ang"], writes=["tq"])
            S.op("dve", lambda h: h.tensor_copy(out=st_i, in_=tq), reads=["tq"], writes=["ki"])
            S.op("dve", lambda h: h.tensor_copy(out=kf, in_=st_i), reads=["ki"], writes=["kf"])
            S.op("dve", lambda h: h.scalar_tensor_tensor(out=tq, in0=kf, scalar=-TWO_PI_HI, in1=ang,
                                                         op0=ALU.mult, op1=ALU.add), reads=["kf", "ang"], writes=["tq"])
            S.op("dve", lambda h: h.scalar_tensor_tensor(out=pf, in0=kf, scalar=-TWO_PI_LO, in1=tq,
                                                         op0=ALU.mult, op1=ALU.add), reads=["kf", "tq"], writes=["pf"])
            S.op("dve", lambda h: h.tensor_scalar(out=pf, in0=pf, scalar1=-3.14159, scalar2=3.14159,
                                                  op0=ALU.max, op1=ALU.min), reads=["pf"], writes=["pf"])
            S.op("act", lambda h, sl=sl: h.activation(out=sinT[:, sl], in_=pf, func=AF.Sin),
                 reads=["pf"], writes=[f"sin{tc}"])
            S.op("act", lambda h: h.activation(out=ang, in_=pf, func=AF.Abs),
                 reads=["pf"], writes=["ang"])
            S.op("act", lambda h, sl=sl: h.activation(out=cosT[:, sl], in_=ang, func=AF.Sin, scale=-1.0,
                                                      bias=float(np.pi / 2)), reads=["ang"], writes=[f"cos{tc}"])

        ps_bf3 = psb[3][:, :].bitcast(BF16)
        SCL = float(128 ** -0.5)

        def rope(tc, pa, pb, dstA, dstB, tokA, tokB, qchunk=None):
            sl = slice(tc * 512, (tc + 1) * 512)
            C, Sn = cosT[:, sl], sinT[:, sl]
            ta, tb, tcc, td = rt
            S.op("dve", lambda h: h.tensor_tensor(out=ta, in0=pa[0], in1=C, op=ALU.mult),
                 reads=[pa[1], f"cos{tc}"], writes=["rt0"])
            S.op("dve", lambda h: h.tensor_tensor(out=tb, in0=pb[0], in1=Sn, op=ALU.mult),
                 reads=[pb[1], f"sin{tc}"], writes=["rt1"])
            S.op("dve", lambda h: h.tensor_tensor(out=tcc, in0=pb[0], in1=C, op=ALU.mult),
                 reads=[pb[1], f"cos{tc}"], writes=["rt2"])
            S.op("dve", lambda h: h.tensor_tensor(out=td, in0=pa[0], in1=Sn, op=ALU.mult),
                 reads=[pa[1], f"sin{tc}"], writes=["rt3"])
            if qchunk is None:
                S.op("dve", lambda h: h.tensor_tensor(out=dstA, in0=ta, in1=tb, op=ALU.subtract),
                     reads=["rt0", "rt1"], writes=[tokA])
                S.op("dve", lambda h: h.tensor_tensor(out=dstB, in0=tcc, in1=td, op=ALU.add),
                     reads=["rt2", "rt3"], writes=[tokB])
            else:
                for s_, (x0, x1, op_, rds) in enumerate(((ta, tb, ALU.subtract, ["rt0", "rt1"]),
                                                         (tcc, td, ALU.add, ["rt2", "rt3"]))):
                    for hh_ in range(2):
                        pp = slice(64 * hh_, 64 * hh_ + 64)
                        S.op("dve", lambda h, s_=s_, x0=x0, x1=x1, op_=op_, pp=pp, hh_=hh_: h.tensor_tensor(
                            out=Qz[s_][pp, 4 * qchunk:4 * qchunk + 4, hh_, :],
                            in0=x0[pp, :].rearrange("p (a b) -> p a b", b=128),
                            in1=x1[pp, :].rearrange("p (a b) -> p a b", b=128), op=op_),
                            reads=rds + [f"Qz{s_}"], writes=[f"Q{qchunk}"])

        def proj_group(bank, wv, wtok, col0, xsrc, xtok, ncols=128, ntok=512):
            def fn(h):
                ins = None
                for dc in range(16):
                    ins = h.matmul(psb[bank][:, 0:ntok], lhsT=wv[:, dc, col0:col0 + 128], rhs=xsrc[:, dc, :],
                                   start=(dc == 0), stop=(dc == 15))
                return ins
            S.op("pe", fn, reads=[wtok] + list(xtok), writes=[f"ps{bank}"])

        def emit_xh(n, after=()):
            if n >= 16:
                return
            tcn = n % 4
            sn = n % 2
            S.dma("pool", f"xh{sn}", lambda h, sn=sn, tcn=tcn: [h.dma_start(
                out=xh[sn], in_=xT_v[:, :, tcn * 512:(tcn + 1) * 512])], reads=list(after), writes=[f"xh{sn}"])
        emit_xh(0)
        W._emit_load(0)
        W._emit_load(1)
        emit_xh(1)
        W._emit_load(2, after=["xh1"])
        W._emit_load(3, after=["xh1"])
        W.emitted = 4
        pbank = 0
        att_units = []
        for g in range(4):
            wk, wk_t = W.use()
            wv_, wv_t = W.use()
            wq, wq_t = W.use()
            for tc in range(6):
                if tc < 4:
                    s = (4 * g + tc) % 2
                    xsrc, xtok = xh[s], [f"xh{s}"]
                else:
                    hf = tc - 4
                    xsrc, xtok = xTo[:, :, hf * 512:(hf + 1) * 512], [f"xTo{hf}"]
                ba, bb = pbank, pbank + 1
                pbank = (pbank + 2) % 4
                proj_group(ba, wk, wk_t, 0, xsrc, xtok)
                proj_group(bb, wk, wk_t, 128, xsrc, xtok)
                sl = slice(tc * 512, (tc + 1) * 512)
                rope(tc, (psb[ba][:, :], f"ps{ba}"), (psb[bb][:, :], f"ps{bb}"), KT[0][:, sl], KT[1][:, sl],
                     f"KA{tc}", f"KB{tc}")
                if tc >= 4:
                    ba, bb = pbank, pbank + 1
                    pbank = (pbank + 2) % 4
                    proj_group(ba, wq, wq_t, 0, xsrc, xtok)
                    proj_group(bb, wq, wq_t, 128, xsrc, xtok)
                    rope(tc, (psb[ba][:, :], f"ps{ba}"), (psb[bb][:, :], f"ps{bb}"), None, None, None, None,
                         qchunk=tc - 4)
                for tt in range(4):
                    kt = tc * 4 + tt
                    vb = 4 + (kt % 2)

                    def vfn(h, tt=tt, vb=vb, xsrc=xsrc, wv_=wv_):
                        ins = None
                        for dc in range(16):
                            ins = h.matmul(psb[vb][:, 0:256], lhsT=xsrc[:, dc, tt * 128:(tt + 1) * 128],
                                           rhs=wv_[:, dc, 0:256], start=(dc == 0), stop=(dc == 15))
                        return ins
                    S.op("pe", vfn, reads=[wv_t] + xtok, writes=[f"ps{vb}"])
                    S.op("act", lambda h, kt=kt, vb=vb: h.activation(
                        out=Vaug[:, kt, :, 0:128], in_=psb[vb][:, 0:256].rearrange("p (a b) -> p a b", b=128),
                        func=AF.Copy), reads=[f"ps{vb}"], writes=[f"V{kt}"])
                if tc < 4:
                    emit_xh(4 * g + tc + 2)
                    if g == 0 and tc == 1:
                        deferred_loads()

            W.release(3)
            units = [(qb, grp) for qb in range(8) for grp in range(9)]
            sbank_i = [0]
            ep_i = [0]
            state = {}

            def emit_qk(idx):
                qb, grp = units[idx]
                b = 16 + qb
                d0 = 2 * grp
                n = 2 if grp < 8 else 1
                bank = sbank_i[0] % 4
                sbank_i[0] += 1
                kts = [b - (d0 + i) for i in range(n)]
                rd = []
                for kt in kts:
                    rd += [f"KA{kt // 4}", f"KB{kt // 4}"]
                rd += [f"Q{qb // 4}"]
                def fn(h, kts=kts, bank=bank, qb=qb):
                    ins = None
                    for i, kt in enumerate(kts):
                        for s_ in range(2):
                            ins = h.matmul(psb[bank][:, i * 256:(i + 1) * 256],
                                           lhsT=KT[s_][:, kt * 128:(kt + 1) * 128],
                                           rhs=Qz[s_][:, qb, :, :].rearrange("p a b -> p (a b)"),
                                           start=(s_ == 0), stop=(s_ == 1))
                    return ins
                S.op("pe", fn, reads=list(dict.fromkeys(rd)), writes=[f"ps{bank}"])
                e = ep_i[0] % 4
                ep_i[0] += 1
                w = 2 * n * 128
                S.op("act", lambda h, bank=bank, w=w, e=e, n=n: h.activation(
                    out=Ebuf[e][:, 0:w].rearrange("p (a i q) -> p i a q", a=2, i=n),
                    in_=psb[bank][:, 0:w].rearrange("p (i a q) -> p i a q", a=2, i=n), func=AF.Exp, scale=SCL),
                    reads=[f"ps{bank}"], writes=[f"E{e}"])
                for hh_ in range(2):
                    S.op("dve", lambda h, n=n, e=e, d0=d0, hh_=hh_: h.tensor_tensor(
                        out=PTb[e][:, hh_ * n * 128:(hh_ + 1) * n * 128],
                        in0=Ebuf[e][:, hh_ * n * 128:(hh_ + 1) * n * 128],
                        in1=masks[:, d0 * 128:(d0 + n) * 128], op=ALU.mult),
                        reads=[f"E{e}", "masks"], writes=[f"PT{e}_{hh_}"])
                state[idx] = (e, kts, n)

            def emit_pv(idx):
                qb, grp = units[idx]
                e, kts, n = state.pop(idx)
                first = grp == 0
                last = grp == 8
                for hh_ in range(2):
                    ob = 4 + 2 * (qb % 2) + hh_

                    def fn(h, kts=kts, e=e, ob=ob, hh_=hh_, n=n):
                        ins = None
                        for i, kt in enumerate(kts):
                            ins = h.matmul(psb[ob][:, 0:129], lhsT=PTb[e][:, (hh_ * n + i) * 128:(hh_ * n + i + 1) * 128],
                                           rhs=Vaug[:, kt, hh_, :], start=(first and i == 0),
                                           stop=(last and i == len(kts) - 1))
                        return ins
                    S.op("pe", fn, reads=[f"PT{e}_{hh_}", f"Vval{hh_}"] + [f"V{kt}" for kt in kts],
                         writes=[f"ps{ob}"])
                    if last:
                        o = 2 * (qb % 2) + hh_
                        head = 2 * g + hh_
                        S.op("dve", lambda h, ob=ob, o=o: h.reciprocal(out=rLb[o][:, 0:1], in_=psb[ob][:, 128:129]),
                             reads=[f"ps{ob}"], writes=[f"rL{o}"])
                        S.op("dve", lambda h, ob=ob, o=o: h.tensor_scalar(
                            out=Onb[o], in0=psb[ob][:, 0:128], scalar1=rLb[o][:, 0:1], scalar2=None, op0=ALU.mult),
                            reads=[f"ps{ob}", f"rL{o}"], writes=[f"On{o}"])
                        tview = psb[ob][:, :].bitcast(BF16)[:, 512:640]

                        def late_fn(o=o, tview=tview, head=head, qb=qb, ob=ob):
                            S.op("pe", lambda h: h.transpose(out=tview, in_=Onb[o], identity=ident),
                                 reads=[f"On{o}", "ident"], writes=[f"ps{ob}"])
                            S.op("act", lambda h: h.activation(
                                out=OT[:, head, qb * 128:(qb + 1) * 128], in_=tview,
                                func=AF.Copy), reads=[f"ps{ob}"], writes=[f"OT{head}_{qb}"])
                        late.append((idx + 4, late_fn))

            ADEPTH = 3
            late = []

            def flush_late(now):
                while late and late[0][0] <= now:
                    late.pop(0)[1]()
            for idx in range(len(units)):
                emit_qk(idx)
                if idx >= ADEPTH:
                    emit_pv(idx - ADEPTH)
                    flush_late(idx - ADEPTH)
            for idx in range(max(0, len(units) - ADEPTH), len(units)):
                emit_pv(idx)
            flush_late(10 ** 9)

        if debug:
            S.dma("sp", "dbg", lambda h: [h.dma_start(out=dbg["OT"][:, :], in_=OT.rearrange("p a b -> p (a b)"))],
                  reads=[f"OT{hd}_{qb}" for hd in range(8) for qb in range(8)], writes=["dbgOT"])
        S.barrier()

        Cb = Bump(92672)
        mergedT = Cb.take((16, TOK), BF16)
        xh16 = Cb.take((16, 16), BF16)
        ubuf = [Cb.take((1040,), F32) for _ in range(3)]
        poolT = Cb.take((8, TOK), BF16)
        ypT = Cb.take((8, TOK), BF16)
        dtmp = [[Cb.take((512,), F32) for _ in range(4)] for _ in range(2)]
        t16 = Cb.take((16,), F32)
        assert Cb.o <= ARENA_ELEMS * 2

        S.dma("pool", "xh16", lambda h: [h.dma_start(out=xh16, in_=xT_v[:, :, HALO - 16:HALO])], writes=["xh16"])
        xo_tok = ["xTo0", "xTo1"]
        for j in range(8):
            if j % 2 == 0:
                wu, wu_t = W.use()
            c0 = (j % 2) * 128
            g = j // 2
            wwin = 2 ** (g + 1)
            for hf in range(2):
                proj_group(hf, wu, wu_t, c0, xTo[:, :, hf * 512:(hf + 1) * 512], [])

            def hfn(h, wu=wu, c0=c0):
                ins = None
                for dc in range(16):
                    ins = h.matmul(psb[2][:, 0:16], lhsT=wu[:, dc, c0:c0 + 128], rhs=xh16[:, dc, :],
                                   start=(dc == 0), stop=(dc == 15))
                return ins
            S.op("pe", hfn, reads=[wu_t, "xh16"], writes=["ps2"])
            u = ubuf[0]
            S.op("act", lambda h, u=u: h.activation(out=u[:, 16:528], in_=psb[0][:, :], func=AF.Copy),
                 reads=["ps0"], writes=["u_a"])
            S.op("act", lambda h, u=u: h.activation(out=u[:, 528:1040], in_=psb[1][:, :], func=AF.Copy),
                 reads=["ps1"], writes=["u_b"])
            S.op("act", lambda h, u=u: h.activation(out=u[:, 0:16], in_=psb[2][:, 0:16], func=AF.Copy),
                 reads=["ps2"], writes=["u_c"])
            src, srct = u, ["u_a", "u_b", "u_c"]
            for k in range(1, g + 2):
                dst = ubuf[1 + (k % 2)]
                dtok = f"ub{1 + (k % 2)}"
                t0 = 2 ** k - 1
                sh = 2 ** (k - 1)
                S.op("dve", lambda h, dst=dst, src=src, t0=t0, sh=sh: h.tensor_tensor(
                    out=dst[:, t0:1040], in0=src[:, t0:1040], in1=src[:, t0 - sh:1040 - sh], op=ALU.add),
                    reads=srct, writes=[dtok])
                src, srct = dst, [dtok]
            S.op("dve", lambda h, src=src, u=u, j=j, wwin=wwin: h.scalar_tensor_tensor(
                out=poolT[:, j, 16:TOK], in0=src[:, 32:1040], scalar=float(1.0 / wwin), in1=u[:, 32:1040],
                op0=ALU.mult, op1=ALU.subtract), reads=srct + ["u_a", "u_b"], writes=[f"pl{j}"])
            S.op("dve", lambda h, src=src, g=g: h.tensor_tensor(
                out=t16, in0=src[:, 16:32], in1=cvec[:, CV_INVC + g * 16:CV_INVC + (g + 1) * 16], op=ALU.mult),
                reads=srct, writes=["t16"])
            S.op("dve", lambda h, u=u, j=j: h.tensor_tensor(
                out=poolT[:, j, 0:16], in0=t16, in1=u[:, 16:32], op=ALU.subtract),
                reads=["t16", "u_a"], writes=[f"plh{j}"])
            if j % 2 == 1:
                W.release(1)
        for g in range(4):
            for oc in range(2):
                for hf in range(2):
                    bank = 4 + (g * 4 + oc * 2 + hf) % 2

                    def pfn(h, g=g, oc=oc, hf=hf, bank=bank):
                        ins = None
                        for kc in range(2):
                            ins = h.matmul(psb[bank][:, :], lhsT=wpool_sb[:, g, kc, oc * 128:(oc + 1) * 128],
                                           rhs=poolT[:, 2 * g + kc, hf * 512:(hf + 1) * 512],
                                           start=(kc == 0), stop=(kc == 1))
                        return ins
                    S.op("pe", pfn, reads=[f"pl{2 * g}", f"pl{2 * g + 1}", f"plh{2 * g}", f"plh{2 * g + 1}"],
                         writes=[f"ps{bank}"])
                    jo = 2 * g + oc
                    S.op("act", lambda h, jo=jo, hf=hf, bank=bank: h.activation(
                        out=ypT[:, jo, hf * 512:(hf + 1) * 512], in_=psb[bank][:, :], func=AF.Copy,
                        scale=cvec[:, CV_PSC + jo:CV_PSC + jo + 1]), reads=[f"ps{bank}"], writes=[f"yp{jo}_{hf}"])
        if debug:
            S.dma("sp", "dbg", lambda h: [h.dma_start(out=dbg["yp"][:, :], in_=ypT.rearrange("p a b -> p (a b)"))],
                  reads=[f"yp{jo}_{hf}" for jo in range(8) for hf in range(2)], writes=["dbgyp"])

        for j in range(16):
            wg, wg_t = W.use()
            wb, wb_t = W.use()
            for hf in range(2):
                bs = 4 * hf
                tsl = slice(hf * 512, (hf + 1) * 512)
                proj_group(bs + 0, wg, wg_t, 0, xTo[:, :, tsl], [])
                proj_group(bs + 1, wg, wg_t, 128, xTo[:, :, tsl], [])

                def yfn(h, bank, base, src, wb=wb, tsl=tsl):
                    ins = None
                    for c in range(8):
                        ins = h.matmul(psb[bank][:, :], lhsT=wb[:, base + c, 0:128], rhs=src[:, c, tsl],
                                       start=(c == 0), stop=(c == 7))
                    return ins
                S.op("pe", lambda h, bs=bs, yfn=yfn: yfn(h, bs + 2, 0, OT), reads=[wb_t, "OTreg"],
                     writes=[f"ps{bs + 2}"])
                S.op("pe", lambda h, bs=bs, yfn=yfn: yfn(h, bs + 3, 8, ypT),
                     reads=[wb_t] + [f"yp{jo}_{hf}" for jo in range(8)], writes=[f"ps{bs + 3}"])
                sA, sB, m1, m2 = dtmp[hf]
                S.op("act", lambda h, bs=bs, sA=sA: h.activation(out=sA, in_=psb[bs][:, :], func=AF.Sigmoid),
                     reads=[f"ps{bs}"], writes=[f"sA{hf}"])
                S.op("act", lambda h, bs=bs, sB=sB: h.activation(out=sB, in_=psb[bs + 1][:, :], func=AF.Sigmoid),
                     reads=[f"ps{bs + 1}"], writes=[f"sB{hf}"])
                S.op("dve", lambda h, bs=bs, sA=sA, m1=m1: h.tensor_tensor(out=m1, in0=sA, in1=psb[bs + 2][:, :],
                                                                          op=ALU.mult),
                     reads=[f"sA{hf}", f"ps{bs + 2}"], writes=[f"m1{hf}"])
                S.op("dve", lambda h, bs=bs, sB=sB, m2=m2: h.tensor_tensor(out=m2, in0=sB, in1=psb[bs + 3][:, :],
                                                                          op=ALU.mult),
                     reads=[f"sB{hf}", f"ps{bs + 3}"], writes=[f"m2{hf}"])
                S.op("dve", lambda h, m1=m1, m2=m2, j=j, tsl=tsl: h.tensor_tensor(
                    out=mergedT[:, j, tsl], in0=m1, in1=m2, op=ALU.add),
                    reads=[f"m1{hf}", f"m2{hf}"], writes=[f"mg{j}_{hf}"])
            W.release(2)
        if debug:
            S.dma("sp", "dbg", lambda h: [h.dma_start(out=dbg["mg"][:, :], in_=mergedT.rearrange("p a b -> p (a b)"))],
                  reads=[f"mg{j}_{hf}" for j in range(16) for hf in range(2)], writes=["dbgmg"])

        x1f = carve(125440, (16, TOK), F32)
        x1b = carve(43520, (16, TOK), BF16)
        Eb = Bump(76288)
        xf = [Eb.take((512,), F32) for _ in range(4)]
        sqb = [Eb.take((512,), F32) for _ in range(2)]
        SS = [[None, None], [None, None]]
        SS[0][0] = Eb.take((512,), F32)
        SS[0][1] = Eb.take((512,), F32)
        assert Eb.o <= 92672
        Ub = Bump(190976)
        lnA = Ub.take((TOK,), F32)
        lnB = Ub.take((TOK,), F32)
        lt = [Ub.take((512,), F32) for _ in range(2)]
        SS[1][0] = Ub.take((512,), F32)
        SS[1][1] = Ub.take((512,), F32)
        assert Ub.o <= ARENA_ELEMS * 2

        def stats_accum(hf, j, zap, ztok):
            s1, s2 = SS[hf]
            t1, t2 = f"S1_{hf}", f"S2_{hf}"
            if j == 0:
                S.op("act", lambda h: h.activation(out=s1, in_=zap, func=AF.Copy), reads=[ztok], writes=[t1])
                S.op("act", lambda h: h.activation(out=s2, in_=zap, func=AF.Square), reads=[ztok], writes=[t2])
            else:
                q = j % 2
                S.op("dve", lambda h: h.tensor_tensor(out=s1, in0=s1, in1=zap, op=ALU.add),
                     reads=[ztok, t1], writes=[t1])
                S.op("act", lambda h: h.activation(out=sqb[q], in_=zap, func=AF.Square),
                     reads=[ztok], writes=[f"sq{q}"])
                S.op("dve", lambda h: h.tensor_tensor(out=s2, in0=s2, in1=sqb[q], op=ALU.add),
                     reads=[f"sq{q}", t2], writes=[t2])

        def stats_final(hf, b1, b2):
            s1, s2 = SS[hf]
            S.op("pe", lambda h: h.matmul(psb[b1][:, :], lhsT=ones_f, rhs=s1, start=True, stop=True),
                 reads=[f"S1_{hf}", "ones_f"], writes=[f"ps{b1}"])
            S.op("pe", lambda h: h.matmul(psb[b2][:, :], lhsT=ones_f, rhs=s2, start=True, stop=True),
                 reads=[f"S2_{hf}", "ones_f"], writes=[f"ps{b2}"])

        def ln_tables(hf, bsum, bsq):
            tsl = slice(hf * 512, (hf + 1) * 512)
            mean, ex2 = lt
            S.op("act", lambda h: h.activation(out=mean, in_=psb[bsum][:, :], func=AF.Copy, scale=float(1.0 / D)),
                 reads=[f"ps{bsum}"], writes=["lt0"])
            S.op("act", lambda h: h.activation(out=ex2, in_=psb[bsq][:, :], func=AF.Copy, scale=float(1.0 / D)),
                 reads=[f"ps{bsq}"], writes=["lt1"])
            S.op("dve", lambda h: h.tensor_tensor(out=lnB[:, tsl], in0=mean, in1=mean, op=ALU.mult),
                 reads=["lt0"], writes=[f"lnB{hf}"])
            S.op("dve", lambda h: h.tensor_tensor(out=ex2, in0=ex2, in1=lnB[:, tsl], op=ALU.subtract),
                 reads=["lt1", f"lnB{hf}"], writes=["lt1"])
            S.op("dve", lambda h: h.tensor_scalar(out=ex2, in0=ex2, scalar1=float(LN_EPS), scalar2=None,
                                                  op0=ALU.add), reads=["lt1"], writes=["lt1"])
            S.op("act", lambda h: h.activation(out=ex2, in_=ex2, func=AF.Sqrt), reads=["lt1"], writes=["lt1"])
            S.op("dve", lambda h: h.reciprocal(out=lnA[:, tsl], in_=ex2), reads=["lt1"], writes=[f"lnA{hf}"])
            S.op("dve", lambda h: h.scalar_tensor_tensor(out=lnB[:, tsl], in0=mean, scalar=-1.0, in1=lnA[:, tsl],
                                                         op0=ALU.mult, op1=ALU.mult),
                 reads=["lt0", f"lnA{hf}"], writes=[f"lnB{hf}"])

        def ln_apply(j, hf, buf, ztok, gcol, bcol, out_tok, bf_out=None):
            tsl = slice(hf * 512, (hf + 1) * 512)
            zz = buf[:, j, tsl]
            g_ap = cvec[:, gcol + j:gcol + j + 1]
            b_ap = cvec[:, bcol + j:bcol + j + 1]
            S.op("dve", lambda h: h.tensor_tensor(out=zz, in0=zz, in1=lnA[:, tsl], op=ALU.mult),
                 reads=[ztok, f"lnA{hf}"], writes=[ztok])
            S.op("dve", lambda h: h.tensor_tensor(out=zz, in0=zz, in1=lnB[:, tsl], op=ALU.add),
                 reads=[ztok, f"lnB{hf}"], writes=[ztok])
            if bf_out is not None:
                S.op("act", lambda h: h.activation(out=bf_out[:, j, tsl], in_=zz, func=AF.Identity,
                                                   scale=g_ap, bias=b_ap),
                     reads=[ztok], writes=[out_tok])
            S.op("act", lambda h: h.activation(out=zz, in_=zz, func=AF.Identity, scale=g_ap, bias=b_ap),
                 reads=[ztok] + ([out_tok] if bf_out is not None else []), writes=[ztok])

        def stats_mm(bank, src, srctok, first, last):
            S.op("pe", lambda h: h.matmul(psb[bank][:, :], lhsT=ones_f, rhs=src, start=first, stop=last),
                 reads=[srctok], writes=[f"ps{bank}"])

        import collections
        pending = collections.deque()

        def drain(n):
            for _ in range(n):
                if pending:
                    pending.popleft()()

        e_iters = [(hf, jp, jj) for hf in range(2) for jp in range(8) for jj in range(2)]

        def ln1_sched(hf):
            stats_final(hf, 4 + hf, 6 + hf)
            pending.append(lambda: ln_tables(hf, 4 + hf, 6 + hf))
            for jx in range(16):
                pending.append(lambda jx=jx: ln_apply(jx, hf, x1f, f"z{jx}_{hf}", CV_LN1G, CV_LN1B,
                                                      f"x1b{jx}_{hf}", bf_out=x1b))

        def emit_xf(i):
            hf, jp, jj = e_iters[i]
            j = 2 * jp + jj
            s = i % 4
            S.dma("sp", f"xf{s}", lambda h, s=s, j=j, hf=hf: [h.dma_start(
                out=xf[s], in_=xT[j * 128:(j + 1) * 128, HALO + hf * 512: HALO + (hf + 1) * 512])],
                writes=[f"xf{s}"] + (["OTreg"] if i < 4 else []))
        emit_xf(0)
        emit_xf(1)
        wo = wo_t = None
        for i, (hf, jp, jj) in enumerate(e_iters):
            if jj == 0:
                wo, wo_t = W.use()
            if i + 2 < len(e_iters):
                emit_xf(i + 2)
            j = 2 * jp + jj
            tsl = slice(hf * 512, (hf + 1) * 512)
            bank = i % 4

            def mfn(h, wo=wo, jj=jj, tsl=tsl, bank=bank):
                ins = None
                for kc in range(16):
                    ins = h.matmul(psb[bank][:, :], lhsT=wo[:, kc, jj * 128:(jj + 1) * 128], rhs=mergedT[:, kc, tsl],
                                   start=(kc == 0), stop=(kc == 15))
                return ins
            S.op("pe", mfn, reads=[wo_t] + [f"mg{kc}_{hf}" for kc in range(16)], writes=[f"ps{bank}"])
            s_ = i % 4
            ztok = f"z{j}_{hf}"
            S.op("dve", lambda h, s_=s_, j=j, tsl=tsl, bank=bank: h.scalar_tensor_tensor(
                out=x1f[:, j, tsl], in0=xf[s_], scalar=ALPHA, in1=psb[bank][:, :], op0=ALU.mult, op1=ALU.add),
                reads=[f"xf{s_}", f"ps{bank}"], writes=[ztok])
            stats_accum(hf, j, x1f[:, j, tsl], ztok)
            if jj == 1:
                W.release(1)
            drain(2 if 18 <= i <= 22 else 1)
            if i == 17:
                ln1_sched(0)
                drain(1)
        drain(len(pending))
        if debug:
            S.dma("sp", "dbg", lambda h: [h.dma_start(out=dbg["x1"][:, :], in_=x1f.rearrange("p a b -> p (a b)"))],
                  reads=[f"z{j}_{hf}" for j in range(16) for hf in range(2)], writes=["dbgx1"])

        hTg = [carve(92672 + b_ * 16384, (8, TOK), BF16) for b_ in range(2)]
        Fb = Bump(76288)
        rb = [Fb.take((512,), F32) for _ in range(2)]
        assert Fb.o <= 84480
        fbank = [0]
        rcnt = [0]

        def nextbank():
            b_ = fbank[0] % 4
            fbank[0] += 1
            return b_

        def f1_one(hg, i, cc, th, w1s, w1_t, ndrain):
            buf = hTg[hg % 2]
            hc = 2 * i + cc
            bank = nextbank()
            tsl = slice(th * 512, (th + 1) * 512)

            def f1(h):
                ins = None
                for kc in range(16):
                    ins = h.matmul(psb[bank][:, :], lhsT=w1s[:, kc, cc * 128:(cc + 1) * 128],
                                   rhs=x1b[:, kc, tsl], start=(kc == 0), stop=(kc == 15))
                return ins
            S.op("pe", f1, reads=[w1_t] + [f"x1b{jx}_{th}" for jx in range(16)], writes=[f"ps{bank}"])
            r = rcnt[0] % 2
            rcnt[0] += 1
            S.op("act", lambda h: h.activation(out=rb[r], in_=psb[bank][:, :], func=AF.Relu),
                 reads=[f"ps{bank}"], writes=[f"r{r}"])
            S.op("dve", lambda h: h.tensor_tensor(out=buf[:, hc, tsl], in0=rb[r], in1=rb[r], op=ALU.mult),
                 reads=[f"r{r}"], writes=[f"h{hg % 2}_{hc}_{th}"])
            drain(ndrain)

        f1_count = [0]

        def emit_F1_half(hg, th):
            for i in range(4):
                w1s, w1_t = W.use()
                for cc in range(2):
                    f1_one(hg, i, cc, th, w1s, w1_t, 1 + (f1_count[0] % 2))
                    f1_count[0] += 1
                    if f1_count[0] == 2:
                        ln1_sched(1)
                        drain(1)
                W.release(1)

        def emit_F1(hg):
            if hg == 0:
                raise AssertionError
            else:
                for i in range(4):
                    w1s, w1_t = W.use()
                    for cc in range(2):
                        for th in range(2):
                            f1_one(hg, i, cc, th, w1s, w1_t, 1)
                    W.release(1)

        def f2_group(hg, j, jj, th, w2s, w2_t):
            buf = hTg[hg % 2]
            bank = nextbank()
            tsl = slice(th * 512, (th + 1) * 512)

            def f2(h):
                ins = None
                for hc in range(8):
                    ins = h.matmul(psb[bank][:, :], lhsT=w2s[:, hc, jj * 128:(jj + 1) * 128],
                                   rhs=buf[:, hc, tsl], start=(hc == 0), stop=(hc == 7))
                return ins
            S.op("pe", f2, reads=[w2_t] + [f"h{hg % 2}_{hc}_{th}" for hc in range(8)], writes=[f"ps{bank}"])
            ztok = f"z{j}_{th}"
            if hg == 0:
                S.op("dve", lambda h: h.scalar_tensor_tensor(
                    out=x1f[:, j, tsl], in0=x1f[:, j, tsl], scalar=ALPHA, in1=psb[bank][:, :],
                    op0=ALU.mult, op1=ALU.add), reads=[f"ps{bank}", ztok], writes=[ztok])
            else:
                S.op("dve", lambda h: h.tensor_tensor(
                    out=x1f[:, j, tsl], in0=x1f[:, j, tsl], in1=psb[bank][:, :], op=ALU.add),
                    reads=[f"ps{bank}", ztok], writes=[ztok])
            if hg == 7:
                stats_accum(th, j, x1f[:, j, tsl], ztok)
                drain(1)

        def ln2_sched(th):
            tsl = slice(th * 512, (th + 1) * 512)
            stats_final(th, 4 + th, 6 + th)
            pending.append(lambda: ln_tables(th, 4 + th, 6 + th))

            def ln2_apply(j):
                ln_apply(j, th, x1f, f"z{j}_{th}", CV_LN2G, CV_LN2B, None)
                if j % 4 == 3:
                    S.dma("sp", "out", lambda h: [h.dma_start(out=outT_v[:, j - 3:j + 1, tsl],
                                                              in_=x1f[:, j - 3:j + 1, tsl])],
                          reads=[f"z{jx}_{th}" for jx in range(j - 3, j + 1)], writes=[f"out{th}_{j // 4}"])
            for j in range(16):
                pending.append(lambda j=j: ln2_apply(j))
            drain(1)

        def emit_F2(hg):
            if hg < 7:
                for jp in range(8):
                    w2s, w2_t = W.use()
                    for jj in range(2):
                        for th in range(2):
                            f2_group(hg, 2 * jp + jj, jj, th, w2s, w2_t)
                    W.release(1)
            else:
                for th in range(2):
                    for jp in range(8):
                        w2s, w2_t = W.use()
                        for jj in range(2):
                            f2_group(hg, 2 * jp + jj, jj, th, w2s, w2_t)
                        W.release(1)
                        if th == 1 and jp == 0:
                            ln2_sched(0)
                ln2_sched(1)

        emit_F1_half(0, 0)
        emit_F1_half(1, 0)
        drain(len(pending))
        emit_F1_half(0, 1)
        emit_F1_half(1, 1)
        emit_F2(0)
        for hg in range(2, 8):
            emit_F1(hg)
            emit_F2(hg - 1)
        emit_F2(7)
        drain(len(pending))
        S.barrier()
        S.emit(block)
        build_nc.last_sched = S
    return nc


def _host_prep(x, positions, w_in, w_pool, pool_scale, w_branch_attn, w_branch_pool, w_out,
               ln_mix_g, ln_mix_b, w_ff1, w_ff2, ln_ff_g, ln_ff_b):
    f32 = np.float32
    x2 = np.asarray(x, f32)[0]
    xTfull = np.ascontiguousarray(x2.T)
    posf = np.asarray(positions, np.int32)[0]
    W = np.asarray(w_in, f32)[0]
    cols = []
    for g in range(4):
        a, b = 2 * g, 2 * g + 1
        for base in (0, 1024):
            A = np.concatenate([np.arange(base + a * 128, base + a * 128 + 64),
                                np.arange(base + b * 128, base + b * 128 + 64)])
            Bc = np.concatenate([np.arange(base + a * 128 + 64, base + a * 128 + 128),
                                 np.arange(base + b * 128 + 64, base + b * 128 + 128)])
            cols += [A, Bc]
        cols.append(np.arange(2048 + a * 128, 2048 + a * 128 + 256))
    w_att = np.ascontiguousarray(W[:, np.concatenate(cols)])
    w_u = np.ascontiguousarray(W[:, 3072:4096])
    gcols = []
    for j in range(16):
        gcols.append(np.arange(4096 + j * 128, 4096 + (j + 1) * 128))
        gcols.append(np.arange(6144 + j * 128, 6144 + (j + 1) * 128))
    w_g = np.ascontiguousarray(W[:, np.concatenate(gcols)])
    w_b = np.ascontiguousarray(np.concatenate([np.asarray(w_branch_attn, f32)[0],
                                               np.asarray(w_branch_pool, f32)[0]], axis=0))
    shared = dict(
        w_att=w_att, w_u=w_u, w_g=w_g, w_b=w_b,
        w_out=np.ascontiguousarray(np.asarray(w_out, f32)[0]),
        w1=np.ascontiguousarray(np.asarray(w_ff1, f32)[0]),
        w2=np.ascontiguousarray(np.asarray(w_ff2, f32)[0]),
        w_pool=np.ascontiguousarray(np.asarray(w_pool, f32)[0]),
    )
    k = np.arange(128)[:, None]
    q = np.arange(128)[None, :]
    masks = np.zeros((128, NDELTA, 128), f32)
    for dl in range(NDELTA):
        dist = 128 * dl + q - k
        m = ((dist >= 0) & (dist <= 128)).astype(f32)
        m += ((dist >= 0) & (dist <= 512) & (dist % 4 == 0)).astype(f32)
        m += ((dist >= 0) & (dist <= 2048) & (dist % 16 == 0)).astype(f32)
        masks[:, dl, :] = m
    shared["masks"] = np.ascontiguousarray(masks.reshape(128, NDELTA * 128))
    shared["ident"] = np.eye(128, dtype=f32)
    half = 64
    inv_freq = np.power(f32(10000.0), -(np.arange(half, dtype=f32) / f32(half))).astype(f32)

    def pj(v, n):
        return np.asarray(v, f32).reshape(n, 128).T

    in_maps = []
    for c in range(NCORES):
        lo = c * TOK - HALO
        xw = np.zeros((D, WIN), f32)
        pw = np.zeros((1, WIN), np.int32)
        s0 = max(lo, 0)
        xw[:, s0 - lo:] = xTfull[:, s0:(c + 1) * TOK]
        pw[0, s0 - lo:] = posf[s0:(c + 1) * TOK]
        cv = np.zeros((128, CV_N), f32)
        cv[:, CV_INVF] = np.concatenate([inv_freq, inv_freq])
        gp = lo + np.arange(NKT)[None, :] * 128 + np.arange(128)[:, None]
        cv[:, CV_VALID:CV_VALID + NKT] = (gp >= 0).astype(f32)
        for g, wwin in enumerate((2, 4, 8, 16)):
            cnt = np.minimum(c * TOK + np.arange(16) + 1, wwin).astype(f32)
            cv[:, CV_INVC + g * 16:CV_INVC + (g + 1) * 16] = (f32(1.0) / cnt)[None, :]
        cv[:, CV_PSC:CV_PSC + 8] = pj(np.asarray(pool_scale)[0], 8)
        cv[:, CV_LN1G:CV_LN1G + 16] = pj(np.asarray(ln_mix_g)[0], 16)
        cv[:, CV_LN1B:CV_LN1B + 16] = pj(np.asarray(ln_mix_b)[0], 16)
        cv[:, CV_LN2G:CV_LN2G + 16] = pj(np.asarray(ln_ff_g)[0], 16)
        cv[:, CV_LN2B:CV_LN2B + 16] = pj(np.asarray(ln_ff_b)[0], 16)
        m = dict(shared)
        m.update(xT=xw, pos=pw, cvec=cv)
        in_maps.append(m)
    return in_maps


def kernel(**inputs):
    in_maps = _host_prep(**inputs)
    nc = build_nc()
    res = run_bass_kernel_spmd(nc, in_maps, core_ids=list(range(NCORES)))
    out = np.empty((1, S_TOT, D), np.float32)
    for c in range(NCORES):
        out[0, c * TOK:(c + 1) * TOK, :] = np.asarray(res.results[c]["outT"], np.float32).T
    return out
```
